# Optimizing a Trainium2 kernel written in Bass

```python
import math
import jax, jax.numpy as jnp
from jax import lax
import numpy as np

D_MODEL = 2048
BATCH = 8
SEQ = 2048
DEPTH = 2

D_MIX = D_MODEL
D_RG = D_MIX // 2
RG_BLOCKS = 8
RG_BLOCK = D_RG // RG_BLOCKS
RG_C = 8.0
CONV_W = 4
H_B = 8
HD_B = (D_MIX - D_RG) // H_B
DSW_PATTERNS = ((128, 1), (512, 4), (2048, 16))
N_BUCKETS = 32
MAX_DIST = 2048
H_C = 4
HD_C = (D_MIX // 2) // H_C
MLSTM_CHUNK = 128
H_D = 8
HD_D = (D_MIX - H_C * HD_C) // H_D
SB_BLOCK = 128
D_FF = 5632
RMS_EPS = 1e-6
N_AB = (DEPTH + 1) // 2
N_CD = DEPTH // 2
W_IN_AB = 2 * D_RG + 3 * H_B * HD_B
W_IN_CD = 4 * H_C * HD_C + 2 * H_C + 3 * H_D * HD_D

kernel_name = 'hybrid_rglru_dilated_mlstm_stickbreaking'

F32 = jnp.float32


def rms_norm(x, g):
    xf = x.astype(F32)
    y = xf * lax.rsqrt(jnp.mean(xf * xf, axis=-1, keepdims=True) + RMS_EPS)
    return (y * g.astype(F32)).astype(x.dtype)


def head_rms(x, g):
    xf = x.astype(F32)
    return xf * lax.rsqrt(jnp.mean(xf * xf, axis=-1, keepdims=True) + RMS_EPS) * g.astype(F32)


def swiglu_ffn(x, w_in, w_out):
    g, u = jnp.split(x @ w_in, 2, axis=-1)
    return (jax.nn.silu(g) * u) @ w_out


def causal_dwconv(x, w, b):
    c = x.shape[-1]
    y = lax.conv_general_dilated(x, w[:, None, :].astype(x.dtype), window_strides=(1,),
                                 padding=[(CONV_W - 1, 0)],
                                 dimension_numbers=('NWC', 'WIO', 'NWC'),
                                 feature_group_count=c)
    return y + b.astype(x.dtype)


def rg_lru(x, w_r, b_r, w_i, b_i, lam):
    bsz, s, _ = x.shape
    xf = x.astype(F32)
    xb = xf.reshape(bsz, s, RG_BLOCKS, RG_BLOCK)
    r = jax.nn.sigmoid(jnp.einsum('bsgc,gcd->bsgd', xb, w_r).reshape(bsz, s, D_RG) + b_r)
    i = jax.nn.sigmoid(jnp.einsum('bsgc,gcd->bsgd', xb, w_i).reshape(bsz, s, D_RG) + b_i)
    log_a = -RG_C * r * jax.nn.softplus(-lam.astype(F32))
    a = jnp.exp(log_a)
    u = jnp.sqrt(-jnp.expm1(2.0 * log_a)) * (i * xf)

    def combine(lhs, rhs):
        a1, b1 = lhs
        a2, b2 = rhs
        return a1 * a2, a2 * b1 + b2

    _, h = lax.associative_scan(combine, (a, u), axis=1)
    return h


def t5_causal_bucket(dist):
    max_exact = N_BUCKETS // 2
    d_f = jnp.maximum(dist, 1).astype(F32)
    large = max_exact + (jnp.log(d_f / max_exact) / math.log(MAX_DIST / max_exact)
                         * (N_BUCKETS - max_exact)).astype(jnp.int32)
    large = jnp.minimum(large, N_BUCKETS - 1)
    return jnp.where(dist < max_exact, dist, large)


def dilated_window_group(q, k, v, rel_bias, window, dilation):
    bsz, s, h, hd = q.shape
    blk = window // dilation
    sub = s // dilation
    nblk = -(-sub // blk)
    pad = nblk * blk - sub

    def to_blocks(t):
        t = t.reshape(bsz, sub, dilation, h, hd).transpose(0, 2, 3, 1, 4)
        t = jnp.pad(t, ((0, 0), (0, 0), (0, 0), (0, pad), (0, 0)))
        return t.reshape(bsz, dilation, h, nblk, blk, hd)

    def with_prev(t):
        prev = jnp.pad(t[:, :, :, :-1], ((0, 0), (0, 0), (0, 0), (1, 0), (0, 0), (0, 0)))
        return jnp.concatenate([prev, t], axis=-2)

    qb = to_blocks(q)
    kw = with_prev(to_blocks(k))
    vw = with_prev(to_blocks(v))
    qi = jnp.arange(blk)[:, None]
    ki = jnp.arange(2 * blk)[None, :]
    rel = blk + qi - ki
    band = (rel >= 0) & (rel <= blk)
    bias = rel_bias[t5_causal_bucket(jnp.clip(rel, 0, blk) * dilation)]
    bias = bias.transpose(2, 0, 1).astype(F32)
    not_before_start = (jnp.arange(nblk)[:, None, None] > 0) | (ki[None] >= blk)
    valid = band[None] & not_before_start
    logits = jnp.einsum('brhnqe,brhnke->brhnqk', qb, kw) / math.sqrt(hd) + bias[:, None]
    logits = jnp.where(valid, logits, -jnp.inf)
    m = jnp.max(logits, axis=-1, keepdims=True)
    p = jnp.exp(logits - m)
    den = jnp.sum(p, axis=-1, keepdims=True)
    o = jnp.einsum('brhnqk,brhnke->brhnqe', p, vw) / den
    lse = (m + jnp.log(den))[..., 0]
    o = o.reshape(bsz, dilation, h, nblk * blk, hd)[:, :, :, :sub]
    o = o.transpose(0, 3, 1, 2, 4).reshape(bsz, s, h, hd)
    lse = lse.reshape(bsz, dilation, h, nblk * blk)[:, :, :, :sub]
    lse = lse.transpose(0, 3, 1, 2).reshape(bsz, s, h)
    return o, lse


def dilated_window_attention(q, k, v, rel_bias):
    outs, lses = [], []
    for window, dilation in DSW_PATTERNS:
        o, l = dilated_window_group(q, k, v, rel_bias, window, dilation)
        outs.append(o)
        lses.append(l)
    wts = jax.nn.softmax(jnp.stack(lses, axis=0), axis=0)
    return jnp.einsum('gbsh,gbshe->bshe', wts, jnp.stack(outs, axis=0))


def mixer_ab(xn, w_in, conv_w, conv_b, w_r, b_r, w_i, b_i, lam, qk_g, rel_bias, w_out):
    bsz, s, _ = xn.shape
    db = H_B * HD_B
    gate, xr, q, k, v = jnp.split(xn @ w_in, [D_RG, 2 * D_RG, 2 * D_RG + db, 2 * D_RG + 2 * db], axis=-1)
    h = rg_lru(causal_dwconv(xr, conv_w, conv_b), w_r, b_r, w_i, b_i, lam)
    y_a = (jax.nn.gelu(gate.astype(F32)) * h).astype(xn.dtype)
    q = head_rms(q.reshape(bsz, s, H_B, HD_B), qk_g[0])
    k = head_rms(k.reshape(bsz, s, H_B, HD_B), qk_g[1])
    v = v.reshape(bsz, s, H_B, HD_B).astype(F32)
    y_b = dilated_window_attention(q, k, v, rel_bias).reshape(bsz, s, db).astype(xn.dtype)
    return jnp.concatenate([y_a, y_b], axis=-1) @ w_out


def mlstm_chunkwise(q, k, v, i_pre, f_pre):
    bsz, s, h, d = q.shape
    L = MLSTM_CHUNK
    nc = s // L

    def chunks(t):
        t = t.astype(F32).reshape(bsz, nc, L, h, *t.shape[3:])
        return jnp.moveaxis(t, 3, 1)

    qc = chunks(q)
    kc = chunks(k) * (d ** -0.5)
    vc = chunks(v)
    ig = chunks(i_pre)
    b = jnp.cumsum(jax.nn.log_sigmoid(chunks(f_pre)), axis=-1)
    causal = jnp.tril(jnp.ones((L, L), dtype=bool))
    log_d = jnp.where(causal, b[..., :, None] - b[..., None, :] + ig[..., None, :], -jnp.inf)
    b_last = b[..., -1]
    g = b_last[..., None] - b + ig
    g_max = jnp.max(g, axis=-1)
    wk = jnp.exp(g - g_max[..., None])[..., None] * kc
    c_loc = jnp.einsum('bhnlk,bhnlv->bhnkv', wk, vc)
    n_loc = jnp.sum(wk, axis=-2)

    def step(carry, xs):
        c_st, n_st, m_st = carry
        c_l, n_l, gm, bl = xs
        m_new = jnp.maximum(bl + m_st, gm)
        decay = jnp.exp(bl + m_st - m_new)
        fresh = jnp.exp(gm - m_new)
        c_new = decay[..., None, None] * c_st + fresh[..., None, None] * c_l
        n_new = decay[..., None] * n_st + fresh[..., None] * n_l
        return (c_new, n_new, m_new), (c_st, n_st, m_st)

    init = (jnp.zeros((bsz, h, d, d), F32), jnp.zeros((bsz, h, d), F32), jnp.zeros((bsz, h), F32))
    xs = tuple(jnp.moveaxis(t, 2, 0) for t in (c_loc, n_loc, g_max, b_last))
    _, (c_in, n_in, m_in) = lax.scan(step, init, xs)
    c_in = jnp.moveaxis(c_in, 0, 2)
    n_in = jnp.moveaxis(n_in, 0, 2)
    m_in = jnp.moveaxis(m_in, 0, 2)
    inter = b + m_in[..., None]
    m_t = jnp.maximum(inter, jnp.max(log_d, axis=-1))
    s_qk = jnp.einsum('bhnld,bhnsd->bhnls', qc, kc) * jnp.exp(log_d - m_t[..., None])
    inter_w = jnp.exp(inter - m_t)
    num = (jnp.einsum('bhnls,bhnsv->bhnlv', s_qk, vc)
           + inter_w[..., None] * jnp.einsum('bhnlk,bhnkv->bhnlv', qc, c_in))
    den = jnp.sum(s_qk, axis=-1) + inter_w * jnp.einsum('bhnlk,bhnk->bhnl', qc, n_in)
    hout = num / jnp.maximum(jnp.abs(den), jnp.exp(-m_t))[..., None]
    return jnp.moveaxis(hout, 1, 3).reshape(bsz, s, h, d)


def stick_breaking_attention(q, k, v):
    bsz, s, h, d = q.shape
    nq = s // SB_BLOCK
    qf = jnp.moveaxis(q.astype(F32), 2, 1) * (d ** -0.5)
    kf = jnp.moveaxis(k.astype(F32), 2, 1)
    vf = jnp.moveaxis(v.astype(F32), 2, 1)
    qblocks = jnp.moveaxis(qf.reshape(bsz, h, nq, SB_BLOCK, d), 2, 0)
    kpos = jnp.arange(s)

    def block(args):
        qb, n = args
        qpos = n * SB_BLOCK + jnp.arange(SB_BLOCK)
        before = kpos[None, :] < qpos[:, None]
        z = jnp.einsum('bhqd,bhkd->bhqk', qb, kf)
        log_stay = jnp.where(before, jax.nn.log_sigmoid(-z), 0.0)
        later = lax.cumsum(log_stay, axis=3, reverse=True) - log_stay
        att = jnp.where(before, jnp.exp(jax.nn.log_sigmoid(z) + later), 0.0)
        return jnp.einsum('bhqk,bhkd->bhqd', att, vf)

    out = lax.map(block, (qblocks, jnp.arange(nq)))
    out = jnp.moveaxis(out, 0, 2).reshape(bsz, h, s, d)
    return jnp.moveaxis(out, 1, 2)


def mixer_cd(xn, w_in, conv_w, conv_b, gate_b, h_gain, w_out):
    bsz, s, _ = xn.shape
    dc = H_C * HD_C
    dd = H_D * HD_D
    cuts = [2 * dc, 3 * dc, 4 * dc, 4 * dc + 2 * H_C, 4 * dc + 2 * H_C + dd, 4 * dc + 2 * H_C + 2 * dd]
    qk_c, v_c, o_c, gates, q_d, k_d, v_d = jnp.split(xn @ w_in, cuts, axis=-1)
    qk_c = jax.nn.silu(causal_dwconv(qk_c, conv_w, conv_b))
    q_c, k_c = jnp.split(qk_c, 2, axis=-1)
    gates = gates.astype(F32)
    i_pre = gates[..., :H_C] + gate_b[0]
    f_pre = gates[..., H_C:] + gate_b[1]
    h_c = mlstm_chunkwise(q_c.reshape(bsz, s, H_C, HD_C), k_c.reshape(bsz, s, H_C, HD_C),
                          v_c.reshape(bsz, s, H_C, HD_C), i_pre, f_pre)
    h_c = head_rms(h_c, h_gain.reshape(H_C, HD_C)).reshape(bsz, s, dc)
    y_c = (jax.nn.sigmoid(o_c.astype(F32)) * h_c).astype(xn.dtype)
    y_d = stick_breaking_attention(q_d.reshape(bsz, s, H_D, HD_D), k_d.reshape(bsz, s, H_D, HD_D),
                                   v_d.reshape(bsz, s, H_D, HD_D))
    y_d = y_d.reshape(bsz, s, dd).astype(xn.dtype)
    return jnp.concatenate([y_c, y_d], axis=-1) @ w_out


def setup_inputs(seed: int = 0) -> dict:
    key = jax.random.key(seed)
    ks = jax.random.split(key, 24)

    def dense(k, shape, fan_in):
        return jax.random.normal(k, shape, F32) * (fan_in ** -0.5)

    def small(k, shape, scale):
        return jax.random.normal(k, shape, F32) * scale

    x = jax.random.normal(ks[0], (BATCH, SEQ, D_MODEL), F32)
    norm_g = 1.0 + small(ks[1], (DEPTH, 3, D_MODEL), 0.05)
    ffn_w_in = dense(ks[2], (DEPTH, 2, D_MODEL, 2 * D_FF), D_MODEL)
    ffn_w_out = dense(ks[3], (DEPTH, 2, D_FF, D_MODEL), D_FF)
    ab_w_in = dense(ks[4], (N_AB, D_MODEL, W_IN_AB), D_MODEL)
    ab_conv_w = dense(ks[5], (N_AB, CONV_W, D_RG), CONV_W)
    ab_conv_b = small(ks[6], (N_AB, D_RG), 0.01)
    rg_w_r = dense(ks[7], (N_AB, RG_BLOCKS, RG_BLOCK, RG_BLOCK), RG_BLOCK)
    rg_b_r = small(ks[8], (N_AB, D_RG), 0.01)
    rg_w_i = dense(ks[9], (N_AB, RG_BLOCKS, RG_BLOCK, RG_BLOCK), RG_BLOCK)
    rg_b_i = small(ks[10], (N_AB, D_RG), 0.01)
    a_c = jax.random.uniform(ks[11], (N_AB, D_RG), F32, minval=0.9, maxval=0.999)
    a0 = a_c ** (1.0 / RG_C)
    rg_lambda = jnp.log(a0) - jnp.log1p(-a0)
    qk_gain = 1.0 + small(ks[12], (N_AB, 2, HD_B), 0.05)
    rel_bias = small(ks[13], (N_BUCKETS, H_B), 0.2)
    ab_w_out = dense(ks[14], (N_AB, D_MIX, D_MODEL), D_MIX)
    cd_w_in = dense(ks[15], (N_CD, D_MODEL, W_IN_CD), D_MODEL)
    cd_conv_w = dense(ks[16], (N_CD, CONV_W, 2 * H_C * HD_C), CONV_W)
    cd_conv_b = small(ks[17], (N_CD, 2 * H_C * HD_C), 0.01)
    ig_b = small(ks[18], (N_CD, H_C), 0.1)
    fg_b = jnp.linspace(3.0, 6.0, H_C, dtype=F32)[None, :] + small(ks[19], (N_CD, H_C), 0.1)
    mlstm_gate_bias = jnp.stack([ig_b, fg_b], axis=1)
    mlstm_h_gain = 1.0 + small(ks[20], (N_CD, H_C * HD_C), 0.05)
    cd_w_out = dense(ks[21], (N_CD, D_MIX, D_MODEL), D_MIX)
    return {'x': x, 'norm_g': norm_g, 'ffn_w_in': ffn_w_in, 'ffn_w_out': ffn_w_out,
            'ab_w_in': ab_w_in, 'ab_conv_w': ab_conv_w, 'ab_conv_b': ab_conv_b,
            'rg_w_r': rg_w_r, 'rg_b_r': rg_b_r, 'rg_w_i': rg_w_i, 'rg_b_i': rg_b_i,
            'rg_lambda': rg_lambda, 'qk_gain': qk_gain, 'rel_bias': rel_bias, 'ab_w_out': ab_w_out,
            'cd_w_in': cd_w_in, 'cd_conv_w': cd_conv_w, 'cd_conv_b': cd_conv_b,
            'mlstm_gate_bias': mlstm_gate_bias, 'mlstm_h_gain': mlstm_h_gain, 'cd_w_out': cd_w_out}


def reference(x, norm_g, ffn_w_in, ffn_w_out, ab_w_in, ab_conv_w, ab_conv_b, rg_w_r, rg_b_r,
              rg_w_i, rg_b_i, rg_lambda, qk_gain, rel_bias, ab_w_out, cd_w_in, cd_conv_w,
              cd_conv_b, mlstm_gate_bias, mlstm_h_gain, cd_w_out):
    for layer in range(DEPTH):
        j = layer // 2
        x = x + 0.5 * swiglu_ffn(rms_norm(x, norm_g[layer, 0]), ffn_w_in[layer, 0], ffn_w_out[layer, 0])
        xn = rms_norm(x, norm_g[layer, 1])
        if layer % 2 == 0:
            y = mixer_ab(xn, ab_w_in[j], ab_conv_w[j], ab_conv_b[j], rg_w_r[j], rg_b_r[j],
                         rg_w_i[j], rg_b_i[j], rg_lambda[j], qk_gain[j], rel_bias, ab_w_out[j])
        else:
            y = mixer_cd(xn, cd_w_in[j], cd_conv_w[j], cd_conv_b[j], mlstm_gate_bias[j],
                         mlstm_h_gain[j], cd_w_out[j])
        x = x + y.astype(x.dtype)
        x = x + 0.5 * swiglu_ffn(rms_norm(x, norm_g[layer, 2]), ffn_w_in[layer, 1], ffn_w_out[layer, 1])
    return x
```

```python
import numpy as np
from contextlib import ExitStack
import concourse.bass as bass
import concourse.mybir as mybir
from concourse.bass_utils import run_bass_kernel_spmd

F32 = mybir.dt.float32
BF16 = mybir.dt.bfloat16
AF = mybir.ActivationFunctionType
ALU = mybir.AluOpType

SEQ = 2048
DM = 2048
DFF = 5632
KC = 16
NCORES = 8
EPS = 1e-6
NEG = -30000.0


class Res:
    __slots__ = ("lw", "rd")

    def __init__(self):
        self.lw = None
        self.rd = []


class Op:
    __slots__ = ("eng", "fn", "dma", "deps", "sig", "idx")

    def __init__(self, eng, fn, dma):
        self.eng = eng
        self.fn = fn
        self.dma = dma
        self.deps = []
        self.sig = None


class Sched:
    def __init__(self, nc, es, strict=True):
        self.nc = nc
        self.strict = strict
        self.engs = {"pe": nc.tensor, "act": nc.scalar, "dve": nc.vector, "pool": nc.gpsimd, "sp": nc.sync}
        nd = {"sp": 40, "pool": 24, "act": 8}
        self.dq = {}
        for q, n in nd.items():
            self.dq[q] = {"sems": [es.enter_context(nc.semaphore(f"dq_{q}_{i}")) for i in range(n)],
                          "cnt": [0] * n, "last": [None] * n, "next": 0}
        self.csem = {e: es.enter_context(nc.semaphore(f"cs_{e}")) for e in ("pe", "act", "dve", "pool")}
        self.bar = es.enter_context(nc.semaphore("phase_bar"))
        self.ccount = {e: 0 for e in self.csem}
        self.waited = {e: {} for e in self.engs}
        self.nphase = 0
        self.ops = []
        self.tot_ops = 0
        self.tot_waits = 0

    def begin_phase(self):
        self.ops = []

    def add(self, eng, fn, reads=(), writes=(), dma=False):
        op = Op(eng, fn(), dma)
        op.idx = len(self.ops)
        deps = set()
        for r in reads:
            if r.lw is not None:
                deps.add(r.lw)
        for w in writes:
            if w.lw is not None:
                deps.add(w.lw)
            deps.update(w.rd)
        for r in reads:
            if not dma:
                r.rd = [x for x in r.rd if not (self.ops[x].eng == eng and not self.ops[x].dma)]
            r.rd.append(op.idx)
        for w in writes:
            w.lw = op.idx
            w.rd = []
        deps.discard(op.idx)
        op.deps = sorted(deps)
        self.ops.append(op)
        return op

    def _wait(self, engname, sem, val):
        key = id(sem)
        if self.waited[engname].get(key, 0) >= val:
            return
        self.engs[engname].wait_ge(sem, val)
        self.waited[engname][key] = val
        self.tot_waits += 1

    def end_phase(self):
        ops = self.ops
        needed = set()
        last_on = {}
        for op in ops:
            nd = []
            for j in op.deps:
                d = ops[j]
                if (not d.dma) and (not op.dma) and d.eng == op.eng:
                    if op.eng == "pe" or not self.strict:
                        continue
                nd.append(j)
            op.deps = nd
            needed.update(nd)
            if not op.dma:
                last_on[op.eng] = op.idx
        needed.update(last_on.values())
        for op in ops:
            q = None
            deps = list(op.deps)
            if op.dma:
                q = self.dq[op.eng]
                m = q["next"]
                q["next"] = (m + 1) % len(q["sems"])
                if q["last"][m] is not None:
                    self._wait(op.eng, q["sems"][m], q["cnt"][m])
            for j in deps:
                sem, val = ops[j].sig
                self._wait(op.eng, sem, val)
            meth, a_, k_ = op.fn
            inst = meth(*a_, **k_)
            if op.dma:
                q["cnt"][m] += 16
                inst.then_inc(q["sems"][m], 16)
                op.sig = (q["sems"][m], q["cnt"][m])
                q["last"][m] = op.idx
            elif op.idx in needed:
                self.ccount[op.eng] += 1
                inst.then_inc(self.csem[op.eng], 1)
                op.sig = (self.csem[op.eng], self.ccount[op.eng])
        for qn, q in self.dq.items():
            for m, sem in enumerate(q["sems"]):
                if q["cnt"][m] > 0:
                    self._wait("sp", sem, q["cnt"][m])
        for e, sem in self.csem.items():
            if self.ccount[e] > 0:
                self._wait("sp", sem, self.ccount[e])
        self.nphase += 1
        self.engs["sp"].sem_inc(self.bar, 1)
        for e in ("pe", "act", "dve", "pool", "sp"):
            self.engs[e].wait_ge(self.bar, self.nphase)
        self.tot_ops += len(ops)
        self.ops = []


class Slots:
    def __init__(self, nc, es, name, shape, dtype, n):
        self.t = [es.enter_context(nc.sbuf_tensor(f"{name}{i}", shape, dtype)) for i in range(n)]
        self.r = [Res() for _ in range(n)]
        self.i = 0

    def next(self):
        s = self.i % len(self.t)
        self.i += 1
        return self.t[s], self.r[s]


def pipeline(items, lookahead):
    handles = {}
    n = len(items)
    for i in range(min(lookahead, n)):
        handles[i] = items[i][0]()
    for i in range(n):
        j = i + lookahead
        if j < n:
            handles[j] = items[j][0]()
        items[i][1](handles.pop(i))


class _RecEng:
    def __init__(self, real):
        self._real = real

    def __getattr__(self, name):
        real = getattr(self._real, name)

        def rec(*a, **k):
            return (real, a, k)
        return rec


class RecNC:
    def __init__(self, nc):
        self._nc = nc
        self.tensor = _RecEng(nc.tensor)
        self.scalar = _RecEng(nc.scalar)
        self.vector = _RecEng(nc.vector)
        self.gpsimd = _RecEng(nc.gpsimd)
        self.sync = _RecEng(nc.sync)

    def __getattr__(self, name):
        return getattr(self._nc, name)

    _uid = [0]

    def sbuf_tensor(self, name, shape, dtype):
        RecNC._uid[0] += 1
        return self._nc.sbuf_tensor(f"{name}_{RecNC._uid[0]}", shape, dtype)

    def psum_tensor(self, name, shape, dtype):
        RecNC._uid[0] += 1
        return self._nc.psum_tensor(f"{name}_{RecNC._uid[0]}", shape, dtype)


class KB:
    def __init__(self, nc, es):
        self.nc = RecNC(nc)
        self.S = Sched(nc, es)
        self.dres = {}
        self.x = XState()

    def dr(self, key):
        r = self.dres.get(key)
        if r is None:
            r = self.dres[key] = Res()
        return r

    def begin(self):
        self.S.begin_phase()
        self.dres = {}
        self.r_smalls, self.r_ones = Res(), Res()

    def end(self):
        self.S.end_phase()


def psum_banks(nc, es, n=8):
    return [es.enter_context(nc.psum_tensor(f"ps{i}", [128, 512], F32)) for i in range(n)], [Res() for _ in range(n)]


class XState:
    def __init__(self):
        self.es = None
        self.XN = None
        self.RS = None
        self.valid = False


def x_open(kb):
    xs = kb.x
    if xs.es is None:
        xs.es = ExitStack()
        xs.XN = xs.es.enter_context(kb.nc.sbuf_tensor("XN", [128, KC, SEQ], BF16))
        xs.RS = xs.es.enter_context(kb.nc.sbuf_tensor("RS", [128, SEQ], F32))
        xs.valid = False


def x_close(kb):
    xs = kb.x
    if xs.es is not None:
        xs.es.close()
        xs.es = None
        xs.valid = False


def sumsq_accumulate(kb, tx, rx, acc, racc, tmps, first):
    nc, S = kb.nc, kb.S
    for tt in range(4):
        sl = slice(tt * 512, (tt + 1) * 512)
        if first:
            S.add("act", lambda: nc.scalar.activation(out=acc[:, sl], in_=tx[:, sl], func=AF.Square), reads=[rx], writes=[racc])
        else:
            tmp, rtmp = tmps.next()
            S.add("act", lambda: nc.scalar.activation(out=tmp[:], in_=tx[:, sl], func=AF.Square), reads=[rx], writes=[rtmp])
            S.add("pool", lambda: nc.gpsimd.tensor_tensor(out=acc[:, sl], in0=acc[:, sl], in1=tmp[:], op=ALU.add), reads=[rtmp, racc], writes=[racc])


def rstd_finalize(kb, acc, racc, accb, raccb, ps, rps, RSTD):
    nc, S = kb.nc, kb.S
    S.add("act", lambda: nc.scalar.copy(out=accb, in_=acc[:]), reads=[racc], writes=[raccb])
    for tt in range(4):
        sl = slice(tt * 512, (tt + 1) * 512)
        S.add("pe", lambda: nc.tensor.matmul(ps[tt][:], lhsT=kb.ones_bf[:], rhs=accb[:, sl], start=True, stop=True), reads=[raccb, kb.r_ones], writes=[rps[tt]])
        S.add("act", lambda: nc.scalar.activation(out=acc[:, sl], in_=ps[tt][:], func=AF.Ln, scale=1.0 / DM, bias=EPS), reads=[rps[tt], racc], writes=[racc])
    S.add("act", lambda: nc.scalar.activation(out=acc[:], in_=acc[:], func=AF.Exp, scale=-0.5), reads=[racc], writes=[racc])
    S.add("sp", lambda: nc.sync.dma_start(out=RSTD, in_=acc[:]), reads=[racc], writes=[kb.dr("rstd")], dma=True)


def produce_next(kb, tx, rx, dc, g_next, rxn, r_rs, tmps):
    nc, S = kb.nc, kb.S
    XN, RS = kb.x.XN, kb.x.RS
    S.add("dve", lambda: nc.vector.tensor_scalar(out=XN[:, dc, :], in0=tx[:], scalar1=g_next[:, dc:dc + 1], scalar2=None, op0=ALU.mult),
          reads=[rx, kb.r_smalls], writes=[rxn[dc]])
    sumsq_accumulate(kb, tx, rx, RS, r_rs, tmps, dc == 0)


def norm_from_dram(kb, x_in, xkey, gcol, rxn, r_rs, xs, sqs, ps, rps, RSTD):
    nc, S = kb.nc, kb.S
    XN, RS = kb.x.XN, kb.x.RS
    def mk(c):
        def load():
            t, r = xs.next()
            S.add("sp", lambda: nc.sync.dma_start(out=t[:], in_=x_in[c * 128:(c + 1) * 128, :]), reads=[kb.dr((xkey, c))], writes=[r], dma=True)
            return t, r
        def comp(h):
            t, r = h
            sq, rsq = sqs.next()
            S.add("act", lambda: nc.scalar.activation(out=sq[:], in_=t[:], func=AF.Square), reads=[r], writes=[rsq])
            for tt in range(4):
                S.add("pe", lambda: nc.tensor.matmul(ps[tt][:], lhsT=kb.ones_bf[:], rhs=sq[:, tt * 512:(tt + 1) * 512], start=(c == 0), stop=(c == KC - 1)),
                      reads=[rsq, kb.r_ones], writes=[rps[tt]])
            S.add("dve", lambda: nc.vector.tensor_scalar(out=XN[:, c, :], in0=t[:], scalar1=gcol[:, c:c + 1], scalar2=None, op0=ALU.mult),
                  reads=[r, kb.r_smalls], writes=[rxn[c]])
        return load, comp
    pipeline([mk(c) for c in range(KC)], 2)
    for tt in range(4):
        sl = slice(tt * 512, (tt + 1) * 512)
        S.add("act", lambda: nc.scalar.activation(out=RS[:, sl], in_=ps[tt][:], func=AF.Ln, scale=1.0 / DM, bias=EPS), reads=[rps[tt]], writes=[r_rs])
    S.add("act", lambda: nc.scalar.activation(out=RS[:], in_=RS[:], func=AF.Exp, scale=-0.5), reads=[r_rs], writes=[r_rs])
    S.add("sp", lambda: nc.sync.dma_start(out=RSTD, in_=RS[:]), reads=[r_rs], writes=[kb.dr("rstd")], dma=True)


def ffn_phase(kb, x_in, xkey_in, x_out, xkey_out, gcol, w_in, w_out, RSTD, g_next=None):
    nc, S = kb.nc, kb.S
    NG, GS = 4, 11
    x_open(kb)
    XN, RS = kb.x.XN, kb.x.RS
    with ExitStack() as es:
        kb.begin()
        ps, rps = psum_banks(nc, es)
        rxn = [Res() for _ in range(KC)]
        r_rs = Res()
        hT = es.enter_context(nc.sbuf_tensor("hT", [128, GS, SEQ], BF16))
        rhT = [Res() for _ in range(GS)]
        xs = Slots(nc, es, "xs", [128, SEQ], F32, 3)
        sgs = Slots(nc, es, "sg", [128, 512], F32, 8)
        win = Slots(nc, es, "win", [128, KC, 128], BF16, 6)
        wout = Slots(nc, es, "wout", [128, GS, 128], BF16, 3)

        class SqS:
            i = 0
            def next(self):
                j = self.i % 2
                self.i += 1
                return hT[:, j, :], rhT[j]
        if not kb.x.valid:
            norm_from_dram(kb, x_in, xkey_in, gcol, rxn, r_rs, xs, SqS(), ps, rps, RSTD)

        w_in_v = w_in.rearrange("(k p) n -> p k n", p=128)
        w_out_v = w_out.rearrange("(f p) n -> p f n", p=128)
        items = []
        for g in range(NG):
            for j in range(GS):
                f = g * GS + j
                def load(f=f):
                    tg, rg = win.next()
                    S.add("pool", lambda: nc.gpsimd.dma_start(out=tg[:], in_=w_in_v[:, :, f * 128:(f + 1) * 128]), writes=[rg], dma=True)
                    tu, ru = win.next()
                    S.add("pool", lambda: nc.gpsimd.dma_start(out=tu[:], in_=w_in_v[:, :, DFF + f * 128:DFF + (f + 1) * 128]), writes=[ru], dma=True)
                    return tg, rg, tu, ru
                def comp(h, j=j):
                    tg, rg, tu, ru = h
                    for half in range(2):
                        b0 = half * 4
                        for k in range(KC):
                            for wi, (wt, wr) in enumerate(((tg, rg), (tu, ru))):
                                for tt in range(2):
                                    b = b0 + wi * 2 + tt
                                    tok = slice(half * 1024 + tt * 512, half * 1024 + (tt + 1) * 512)
                                    S.add("pe", lambda: nc.tensor.matmul(ps[b][:], lhsT=wt[:, k, :], rhs=XN[:, k, tok], start=(k == 0), stop=(k == KC - 1)),
                                          reads=[wr, rxn[k]], writes=[rps[b]])
                        for tt in range(2):
                            bg, bu = b0 + tt, b0 + 2 + tt
                            tok = slice(half * 1024 + tt * 512, half * 1024 + (tt + 1) * 512)
                            sg, rsg = sgs.next()
                            su, rsu = sgs.next()
                            S.add("dve", lambda: nc.vector.tensor_tensor(out=sg[:], in0=ps[bg][:], in1=RS[:, tok], op=ALU.mult), reads=[rps[bg], r_rs], writes=[rsg])
                            S.add("act", lambda: nc.scalar.activation(out=sg[:], in_=sg[:], func=AF.Silu), reads=[rsg], writes=[rsg])
                            S.add("dve", lambda: nc.vector.tensor_tensor(out=su[:], in0=ps[bu][:], in1=RS[:, tok], op=ALU.mult), reads=[rps[bu], r_rs], writes=[rsu])
                            S.add("dve", lambda: nc.vector.tensor_tensor(out=hT[:, j, tok], in0=sg[:], in1=su[:], op=ALU.mult), reads=[rsg, rsu], writes=[rhT[j]])
                items.append((load, comp))
            for dc in range(KC):
                def load(g=g, dc=dc):
                    tw, rw = wout.next()
                    S.add("pool", lambda: nc.gpsimd.dma_start(out=tw[:], in_=w_out_v[:, g * GS:(g + 1) * GS, dc * 128:(dc + 1) * 128]), writes=[rw], dma=True)
                    tx, rx = xs.next()
                    src, key = (x_in, xkey_in) if g == 0 else (x_out, xkey_out)
                    S.add("sp", lambda: nc.sync.dma_start(out=tx[:], in_=src[dc * 128:(dc + 1) * 128, :]), reads=[kb.dr((key, dc))], writes=[rx], dma=True)
                    return tw, rw, tx, rx
                def comp(h, g=g, dc=dc):
                    tw, rw, tx, rx = h
                    b0 = (dc % 2) * 4
                    for j in range(GS):
                        for tt in range(4):
                            S.add("pe", lambda: nc.tensor.matmul(ps[b0 + tt][:], lhsT=tw[:, j, :], rhs=hT[:, j, tt * 512:(tt + 1) * 512], start=(j == 0), stop=(j == GS - 1)),
                                  reads=[rw, rhT[j]], writes=[rps[b0 + tt]])
                    for tt in range(4):
                        sl = slice(tt * 512, (tt + 1) * 512)
                        S.add("dve", lambda: nc.vector.scalar_tensor_tensor(out=tx[:, sl], in0=ps[b0 + tt][:], scalar=0.5, in1=tx[:, sl], op0=ALU.mult, op1=ALU.add),
                              reads=[rps[b0 + tt], rx], writes=[rx])
                    S.add("sp", lambda: nc.sync.dma_start(out=x_out[dc * 128:(dc + 1) * 128, :], in_=tx[:]), reads=[rx], writes=[kb.dr((xkey_out, dc))], dma=True)
                    if g_next is not None and g == NG - 1:
                        produce_next(kb, tx, rx, dc, g_next, rxn, r_rs, sgs)
                items.append((load, comp))
        pipeline(items, 2)
        if g_next is not None:
            rstd_finalize(kb, RS, r_rs, hT[:, 0, :], rhT[0], ps, rps, RSTD)
        kb.end()
    kb.x.valid = g_next is not None


def proj_phase(kb, x_in, xkey, gcol, w, fm_specs, tm_specs, RSTD):
    nc, S = kb.nc, kb.S
    x_open(kb)
    XN, RS = kb.x.XN, kb.x.RS
    with ExitStack() as es:
        kb.begin()
        ps, rps = psum_banks(nc, es)
        rxn = [Res() for _ in range(KC)]
        r_rs = Res()
        wfm = Slots(nc, es, "wfm", [128, KC, 128], BF16, 4)
        wtm = Slots(nc, es, "wtm", [128, KC, 512], BF16, 2)
        o32 = Slots(nc, es, "o32", [128, SEQ], F32, 2)
        o16 = Slots(nc, es, "o16", [128, SEQ], BF16, 2)
        otm = Slots(nc, es, "otm", [128, 512], BF16, 4)
        rcol = es.enter_context(nc.sbuf_tensor("rcol", [128, 16], F32)); r_rcol = Res()
        if not kb.x.valid:
            xs = Slots(nc, es, "xs", [128, SEQ], F32, 3)
            sqs = Slots(nc, es, "sq", [128, SEQ], BF16, 2)
            norm_from_dram(kb, x_in, xkey, gcol, rxn, r_rs, xs, sqs, ps, rps, RSTD)
        S.add("sp", lambda: nc.sync.dma_start(out=rcol[:], in_=RSTD[0:1, :].rearrange("o (b i) -> i (o b)", i=128), allow_slow_non_contiguous=True),
              reads=[kb.dr("rstd")], writes=[r_rcol], dma=True)
        wv = w.rearrange("(k p) n -> p k n", p=128)
        items = []
        cnt = [0]
        for (col0, ncols, dst, dt, scale) in fm_specs:
            def load(col0=col0, ncols=ncols):
                t, r = wfm.next()
                S.add("pool", lambda: nc.gpsimd.dma_start(out=t[:, :, 0:ncols], in_=wv[:, :, col0:col0 + ncols]), writes=[r], dma=True)
                return t, r
            def comp(h, ncols=ncols, dst=dst, dt=dt, scale=scale):
                t, r = h
                b0 = (cnt[0] % 2) * 4
                cnt[0] += 1
                for k in range(KC):
                    for tt in range(4):
                        S.add("pe", lambda: nc.tensor.matmul(ps[b0 + tt][0:ncols, :], lhsT=t[:, k, 0:ncols], rhs=XN[:, k, tt * 512:(tt + 1) * 512],
                                                             start=(k == 0), stop=(k == KC - 1)), reads=[r, rxn[k]], writes=[rps[b0 + tt]])
                ot, ro = (o32 if dt == F32 else o16).next()
                for tt in range(4):
                    sl = slice(tt * 512, (tt + 1) * 512)
                    S.add("dve", lambda: nc.vector.scalar_tensor_tensor(out=ot[0:ncols, sl], in0=ps[b0 + tt][0:ncols, :], scalar=float(scale), in1=RS[0:ncols, sl],
                                                                        op0=ALU.mult, op1=ALU.mult), reads=[rps[b0 + tt], r_rs], writes=[ro])
                S.add("sp", lambda: nc.sync.dma_start(out=dst, in_=ot[0:ncols, :]), reads=[ro], dma=True)
            items.append((load, comp))
        for (col0, dst) in tm_specs:
            def load(col0=col0):
                t, r = wtm.next()
                S.add("pool", lambda: nc.gpsimd.dma_start(out=t[:], in_=wv[:, :, col0:col0 + 512]), writes=[r], dma=True)
                return t, r
            def comp(h, dst=dst):
                t, r = h
                for tb in range(16):
                    b = tb % 8
                    for k in range(KC):
                        S.add("pe", lambda: nc.tensor.matmul(ps[b][:], lhsT=XN[:, k, tb * 128:(tb + 1) * 128], rhs=t[:, k, :], start=(k == 0), stop=(k == KC - 1)),
                              reads=[r, rxn[k]], writes=[rps[b]])
                    ot, ro = otm.next()
                    if tb % 2 == 0:
                        S.add("act", lambda: nc.scalar.activation(out=ot[:], in_=ps[b][:], func=AF.Copy, scale=rcol[:, tb:tb + 1]), reads=[rps[b], r_rcol], writes=[ro])
                    else:
                        S.add("dve", lambda: nc.vector.tensor_scalar(out=ot[:], in0=ps[b][:], scalar1=rcol[:, tb:tb + 1], scalar2=None, op0=ALU.mult),
                              reads=[rps[b], r_rcol], writes=[ro])
                    S.add("sp", lambda: nc.sync.dma_start(out=dst[tb * 128:(tb + 1) * 128, :], in_=ot[:]), reads=[ro], dma=True)
            items.append((load, comp))
        pipeline(items, 1)
        kb.end()
    x_close(kb)


def outproj_phase(kb, x_in, xkey_in, x_out, xkey_out, Y, w, RSTD, g_next=None):
    nc, S = kb.nc, kb.S
    if g_next is not None:
        x_open(kb)
    with ExitStack() as es:
        kb.begin()
        ps, rps = psum_banks(nc, es)
        rxn = [Res() for _ in range(KC)]
        r_rs = Res()
        yb = es.enter_context(nc.sbuf_tensor("yb", [128, KC, SEQ], BF16))
        ryb = [Res() for _ in range(KC)]
        xs = Slots(nc, es, "xs", [128, SEQ], F32, 3)
        ws = Slots(nc, es, "ws", [128, KC, 128], BF16, 3)
        accb = es.enter_context(nc.sbuf_tensor("accb", [128, SEQ], BF16)); raccb = Res()
        sqt = Slots(nc, es, "sqt", [128, 512], F32, 2)
        for c in range(KC):
            S.add("sp", lambda: nc.sync.dma_start(out=yb[:, c, :], in_=Y[c * 128:(c + 1) * 128, :]), writes=[ryb[c]], dma=True)
        wv = w.rearrange("(k p) n -> p k n", p=128)
        items = []
        for dc in range(KC):
            def load(dc=dc):
                tw, rw = ws.next()
                S.add("pool", lambda: nc.gpsimd.dma_start(out=tw[:], in_=wv[:, :, dc * 128:(dc + 1) * 128]), writes=[rw], dma=True)
                tx, rx = xs.next()
                S.add("sp", lambda: nc.sync.dma_start(out=tx[:], in_=x_in[dc * 128:(dc + 1) * 128, :]), writes=[rx], dma=True)
                return tw, rw, tx, rx
            def comp(h, dc=dc):
                tw, rw, tx, rx = h
                b0 = (dc % 2) * 4
                for k in range(KC):
                    for tt in range(4):
                        S.add("pe", lambda: nc.tensor.matmul(ps[b0 + tt][:], lhsT=tw[:, k, :], rhs=yb[:, k, tt * 512:(tt + 1) * 512], start=(k == 0), stop=(k == KC - 1)),
                              reads=[rw, ryb[k]], writes=[rps[b0 + tt]])
                for tt in range(4):
                    sl = slice(tt * 512, (tt + 1) * 512)
                    S.add("dve", lambda: nc.vector.tensor_tensor(out=tx[:, sl], in0=ps[b0 + tt][:], in1=tx[:, sl], op=ALU.add), reads=[rps[b0 + tt], rx], writes=[rx])
                S.add("sp", lambda: nc.sync.dma_start(out=x_out[dc * 128:(dc + 1) * 128, :], in_=tx[:]), reads=[rx], dma=True)
                if g_next is not None:
                    produce_next(kb, tx, rx, dc, g_next, rxn, r_rs, sqt)
            items.append((load, comp))
        pipeline(items, 2)
        if g_next is not None:
            rstd_finalize(kb, kb.x.RS, r_rs, accb[:], raccb, ps, rps, RSTD)
        kb.end()
    kb.x.valid = g_next is not None


def abrg_phase(kb, PROJ, Y, d):
    nc, S = kb.nc, kb.S
    sm = kb.smalls
    T = SEQ
    def smc(name, j):
        o = SM[name][0] + j
        return sm[:, o:o + 1]
    with ExitStack() as es:
        kb.begin()
        ps, rps = psum_banks(nc, es)
        rs = kb.r_smalls
        wr = es.enter_context(nc.sbuf_tensor("wr", [128, 8, 128], BF16))
        wi = es.enter_context(nc.sbuf_tensor("wi", [128, 8, 128], BF16))
        r_wr, r_wi = Res(), Res()
        S.add("pool", lambda: nc.gpsimd.dma_start(out=wr[:], in_=d["rg_w_r"][0].rearrange("g c e -> c g e")), writes=[r_wr], dma=True)
        S.add("pool", lambda: nc.gpsimd.dma_start(out=wi[:], in_=d["rg_w_i"][0].rearrange("g c e -> c g e")), writes=[r_wi], dma=True)
        cl = es.enter_context(nc.sbuf_tensor("cl", [128, 4, 8], F32))
        r_cl = Res()
        lo = SM["lam"][0]
        lam = sm[:, lo:lo + 8]
        S.add("dve", lambda: nc.vector.tensor_scalar(out=cl[:, 0, :], in0=lam, scalar1=-1.0, scalar2=None, op0=ALU.mult), reads=[rs], writes=[r_cl])
        S.add("dve", lambda: nc.vector.tensor_tensor(out=cl[:, 0, :], in0=cl[:, 0, :], in1=lam, op=ALU.max), reads=[rs, r_cl], writes=[r_cl])
        S.add("act", lambda: nc.scalar.activation(out=cl[:, 0, :], in_=cl[:, 0, :], func=AF.Exp, scale=-1.0), reads=[r_cl], writes=[r_cl])
        S.add("act", lambda: nc.scalar.activation(out=cl[:, 0, :], in_=cl[:, 0, :], func=AF.Ln, bias=1.0, scale=1.0), reads=[r_cl], writes=[r_cl])
        S.add("dve", lambda: nc.vector.tensor_scalar(out=cl[:, 1, :], in0=lam, scalar1=-1.0, scalar2=0.0, op0=ALU.mult, op1=ALU.max), reads=[rs, r_cl], writes=[r_cl])
        S.add("dve", lambda: nc.vector.tensor_tensor(out=cl[:, 1, :], in0=cl[:, 1, :], in1=cl[:, 0, :], op=ALU.add), reads=[r_cl], writes=[r_cl])
        S.add("dve", lambda: nc.vector.tensor_scalar(out=cl[:, 2, :], in0=cl[:, 1, :], scalar1=-8.0, scalar2=None, op0=ALU.mult), reads=[r_cl], writes=[r_cl])
        S.add("dve", lambda: nc.vector.tensor_scalar(out=cl[:, 3, :], in0=cl[:, 1, :], scalar1=-16.0, scalar2=None, op0=ALU.mult), reads=[r_cl], writes=[r_cl])
        NS = 3
        gts = Slots(nc, es, "gt", [128, T], F32, NS)
        xps = Slots(nc, es, "xp", [128, T + 4], F32, NS)
        xcs = Slots(nc, es, "xc", [128, T], F32, NS)
        xbs = Slots(nc, es, "xb", [128, T], BF16, 2)
        rrs = Slots(nc, es, "rr", [128, T], F32, NS)
        iis = Slots(nc, es, "ii", [128, T], F32, NS)
        b1s = Slots(nc, es, "b1", [128, T], F32, NS)
        ybs = Slots(nc, es, "yo", [128, T], BF16, 2)
        for i in range(NS):
            S.add("dve", lambda i=i: nc.vector.memset(xps.t[i][:, 0:4], 0.0), writes=[xps.r[i]])

        def stA(u):
            c = u["c"]
            gt, rg = gts.next()
            S.add("sp", lambda: nc.sync.dma_start(out=gt[:], in_=PROJ[c * 128:(c + 1) * 128, :]), writes=[rg], dma=True)
            xp, rx = xps.next()
            S.add("sp", lambda: nc.sync.dma_start(out=xp[:, 4:4 + T], in_=PROJ[1024 + c * 128:1024 + (c + 1) * 128, :]), reads=[rx], writes=[rx], dma=True)
            xc, rxc = xcs.next(); xb, rxb = xbs.next(); rr, rrr = rrs.next(); ii, rii = iis.next()
            u.update(gt=gt, rg=rg, xc=xc, rxc=rxc, rr=rr, rrr=rrr, ii=ii, rii=rii)
            S.add("dve", lambda: nc.vector.tensor_scalar(out=xc[:], in0=xp[:, 4:4 + T], scalar1=smc("cw", c * 4 + 3), scalar2=smc("cb", c),
                                                         op0=ALU.mult, op1=ALU.add), reads=[rx, rs], writes=[rxc])
            for j in range(3):
                S.add("dve", lambda: nc.vector.scalar_tensor_tensor(out=xc[:], in0=xp[:, 1 + j:1 + j + T], scalar=smc("cw", c * 4 + j), in1=xc[:],
                                                                    op0=ALU.mult, op1=ALU.add), reads=[rx, rs, rxc], writes=[rxc])
            S.add("act", lambda: nc.scalar.copy(out=xb[:], in_=xc[:]), reads=[rxc], writes=[rxb])
            for tt in range(4):
                sl = slice(tt * 512, (tt + 1) * 512)
                S.add("pe", lambda: nc.tensor.matmul(ps[tt][:], lhsT=wr[:, c, :], rhs=xb[:, sl], start=True, stop=True), reads=[r_wr, rxb], writes=[rps[tt]])
                S.add("pe", lambda: nc.tensor.matmul(ps[4 + tt][:], lhsT=wi[:, c, :], rhs=xb[:, sl], start=True, stop=True), reads=[r_wi, rxb], writes=[rps[4 + tt]])
            for tt in range(4):
                sl = slice(tt * 512, (tt + 1) * 512)
                S.add("act", lambda: nc.scalar.activation(out=rr[:, sl], in_=ps[tt][:], func=AF.Sigmoid, bias=smc("br", c), scale=1.0), reads=[rps[tt], rs], writes=[rrr])
                S.add("act", lambda: nc.scalar.activation(out=ii[:, sl], in_=ps[4 + tt][:], func=AF.Sigmoid, bias=smc("bi", c), scale=1.0), reads=[rps[4 + tt], rs], writes=[rii])

        def stB(u):
            c, xc, rxc, rr, rrr, ii, rii = u["c"], u["xc"], u["rxc"], u["rr"], u["rrr"], u["ii"], u["rii"]
            b1, rb1 = b1s.next()
            u.update(b1=b1, rb1=rb1)
            S.add("act", lambda: nc.scalar.activation(out=b1[:], in_=rr[:], func=AF.Exp, scale=cl[:, 3, c:c + 1]), reads=[rrr, r_cl], writes=[rb1])
            S.add("act", lambda: nc.scalar.activation(out=rr[:], in_=rr[:], func=AF.Exp, scale=cl[:, 2, c:c + 1]), reads=[rrr, r_cl], writes=[rrr])
            S.add("pool", lambda: nc.gpsimd.tensor_scalar(out=b1[:], in0=b1[:], scalar1=-1.0, scalar2=1.0, op0=ALU.mult, op1=ALU.add), reads=[rb1], writes=[rb1])
            S.add("act", lambda: nc.scalar.activation(out=b1[:], in_=b1[:], func=AF.Sqrt), reads=[rb1], writes=[rb1])
            S.add("pool", lambda: nc.gpsimd.tensor_tensor(out=ii[:], in0=ii[:], in1=b1[:], op=ALU.mult), reads=[rii, rb1], writes=[rii])
            S.add("pool", lambda: nc.gpsimd.tensor_tensor(out=ii[:], in0=ii[:], in1=xc[:], op=ALU.mult), reads=[rii, rxc], writes=[rii])
            S.add("dve", lambda: nc.vector.tensor_tensor_scan(out=b1[:], data0=rr[:], data1=ii[:], initial=0.0, op0=ALU.mult, op1=ALU.add),
                  reads=[rrr, rii, rb1], writes=[rb1])

        def stC(u):
            c, gt, rg, xc, rxc, b1, rb1 = u["c"], u["gt"], u["rg"], u["xc"], u["rxc"], u["b1"], u["rb1"]
            yo, ryo = ybs.next()
            S.add("act", lambda: nc.scalar.activation(out=xc[:], in_=gt[:], func=AF.Gelu_apprx_tanh), reads=[rg, rxc], writes=[rxc])
            S.add("dve", lambda: nc.vector.tensor_tensor(out=yo[:], in0=xc[:], in1=b1[:], op=ALU.mult), reads=[rxc, rb1], writes=[ryo])
            S.add("sp", lambda: nc.sync.dma_start(out=Y[c * 128:(c + 1) * 128, :], in_=yo[:]), reads=[ryo], dma=True)

        skewed([dict(c=c) for c in range(8)], [stA, stB, stC])
        kb.end()


def abattn_phase(kb, PROJ, VT, Y, ZT, d):
    nc, S = kb.nc, kb.S
    sm = kb.smalls
    T = SEQ
    with ExitStack() as es:
        kb.begin()
        rs_ = kb.r_smalls
        ps, rps = psum_banks(nc, es)
        rb = es.enter_context(nc.sbuf_tensor("rb", [33, 8], F32)); r_rb = Res()
        selT = es.enter_context(nc.sbuf_tensor("selT", [33, 1152], F32)); r_sel = Res()
        ones33 = es.enter_context(nc.sbuf_tensor("ones33", [33, 128], F32)); r_o33 = Res()
        gqs = es.enter_context(nc.sbuf_tensor("gqs", [128, 2], F32)); r_gqs = Res()
        lhs = Slots(nc, es, "lh", [33, 128], F32, 2)
        zs = Slots(nc, es, "zs", [128, 1152], F32, 2)
        S.add("dve", lambda: nc.vector.memset(rb[:], 1.0), writes=[r_rb])
        S.add("sp", lambda: nc.sync.dma_start(out=rb[0:32, :], in_=d["rel_bias"]), reads=[r_rb], writes=[r_rb], dma=True)
        S.add("sp", lambda: nc.sync.dma_start(out=selT[:], in_=d["selT"]), writes=[r_sel], dma=True)
        S.add("dve", lambda: nc.vector.memset(ones33[:], 1.0), writes=[r_o33])
        go = SM["gqk"][0]
        S.add("dve", lambda: nc.vector.tensor_scalar(out=gqs[:, 0:1], in0=sm[:, go:go + 1], scalar1=float(128.0 ** -0.5), scalar2=None, op0=ALU.mult),
              reads=[rs_], writes=[r_gqs])
        S.add("dve", lambda: nc.vector.tensor_copy(out=gqs[:, 1:2], in_=sm[:, go + 1:go + 2]), reads=[rs_, r_gqs], writes=[r_gqs])
        for h in range(8):
            lh, rlh = lhs.next()
            S.add("dve", lambda h=h, lh=lh: nc.vector.tensor_scalar(out=lh[:], in0=ones33[:], scalar1=rb[:, h:h + 1], scalar2=None, op0=ALU.mult),
                  reads=[r_o33, r_rb], writes=[rlh])
            z, rz = zs.next()
            for p in range(3):
                S.add("pe", lambda p=p, lh=lh: nc.tensor.matmul(ps[p][:, 0:384], lhsT=lh[:], rhs=selT[:, p * 384:(p + 1) * 384], start=True, stop=True),
                      reads=[rlh, r_sel], writes=[rps[p]])
                S.add("act", lambda p=p, z=z: nc.scalar.copy(out=z[:, p * 384:(p + 1) * 384], in_=ps[p][:, 0:384]), reads=[rps[p]], writes=[rz])
            S.add("sp", lambda h=h, z=z: nc.sync.dma_start(out=ZT[h], in_=z[:]), reads=[rz], writes=[kb.dr(("zt", h))], dma=True)

        qfs = Slots(nc, es, "qf", [128, T], F32, 2)
        sqs = Slots(nc, es, "sq", [128, T], BF16, 2)
        rss = Slots(nc, es, "rs", [128, T], F32, 2)
        qns = Slots(nc, es, "qn", [128, T], BF16, 4)
        vts = Slots(nc, es, "vt", [128, 16, 128], BF16, 6)
        b4s = Slots(nc, es, "b4", [128, 512], F32, 12)
        tmps = Slots(nc, es, "tmp", [128, 512], F32, 4)
        pt3s = Slots(nc, es, "pt3", [128, T], BF16, 2)
        ptgs = Slots(nc, es, "ptg", [128, T], BF16, 3)
        rds = Slots(nc, es, "rd", [128, 512], F32, 2)
        yhs = Slots(nc, es, "yh", [128, T], BF16, 2)
        qkb = [0]
        HD = {}
        QR = {}
        for sl_ in (pt3s, ptgs, yhs):
            for t_ in sl_.t:
                QR[id(t_)] = [Res() for _ in range(4)]

        def prep(u):
            h = u["h"]
            qk = []
            for w_ in range(2):
                t, r = qfs.next()
                S.add("sp", lambda: nc.sync.dma_start(out=t[:], in_=PROJ[2048 + w_ * 1024 + h * 128:2048 + w_ * 1024 + (h + 1) * 128, :]), writes=[r], dma=True)
                qk.append((t, r))
            hs = slice(h * 128, (h + 1) * 128)
            vt1, rv1 = vts.next(); vt2, rv2 = vts.next(); vt3, rv3 = vts.next()
            S.add("sp", lambda: nc.sync.dma_start(out=vt1[:], in_=VT.rearrange("(b i) f -> i b f", i=128)[:, :, hs]), writes=[rv1], dma=True)
            for n in range(4):
                S.add("sp", lambda: nc.sync.dma_start(out=vt2[:, n * 4:(n + 1) * 4, :], in_=VT[512 * n:512 * (n + 1), hs].rearrange("(i r) f -> i r f", r=4)), writes=[rv2], dma=True)
            S.add("sp", lambda: nc.sync.dma_start(out=vt3[:], in_=VT.rearrange("(i r) f -> i r f", r=16)[:, :, hs]), writes=[rv3], dma=True)
            B4 = {}
            for p in range(3):
                for part, base in (("cur", 127), ("prev", 255)):
                    t, r = b4s.next()
                    for rep in range(4):
                        src = bass.AP(ZT.tensor, h * 128 * 1152 + p * 384 + base, [[1151, 128], [1, 128]])
                        S.add("sp", lambda: nc.sync.dma_start(out=t[:, rep * 128:(rep + 1) * 128], in_=src), reads=[kb.dr(("zt", h))], writes=[r], dma=True)
                    B4[(p, part)] = (t, r)
            nrm = []
            for w_ in range(2):
                t, r = qk[w_]
                sq, rsq = sqs.next()
                S.add("act", lambda: nc.scalar.activation(out=sq[:], in_=t[:], func=AF.Square), reads=[r], writes=[rsq])
                for tt in range(4):
                    S.add("pe", lambda: nc.tensor.matmul(ps[tt][:], lhsT=kb.ones_bf[:], rhs=sq[:, tt * 512:(tt + 1) * 512], start=True, stop=True),
                          reads=[rsq, kb.r_ones], writes=[rps[tt]])
                rs, rrs = rss.next()
                for tt in range(4):
                    S.add("act", lambda: nc.scalar.activation(out=rs[:, tt * 512:(tt + 1) * 512], in_=ps[tt][:], func=AF.Ln, scale=1.0 / 128, bias=EPS),
                          reads=[rps[tt]], writes=[rrs])
                S.add("act", lambda: nc.scalar.activation(out=rs[:], in_=rs[:], func=AF.Exp, scale=-0.5), reads=[rrs], writes=[rrs])
                qn, rqn = qns.next()
                S.add("dve", lambda: nc.vector.scalar_tensor_tensor(out=qn[:], in0=t[:], scalar=gqs[:, w_:w_ + 1], in1=rs[:], op0=ALU.mult, op1=ALU.mult),
                      reads=[r, rrs, r_gqs], writes=[rqn])
                nrm.append((qn, rqn))
            HD[h] = dict(qn=nrm[0], kn=nrm[1], vt=((vt1, rv1), (vt2, rv2), (vt3, rv3)), B4=B4)

        def qk_batch(h, pairs, bkey, dst, rdst):
            qn, rqn = HD[h]["qn"]; kn, rkn = HD[h]["kn"]
            b = qkb[0] % 4
            qkb[0] += 1
            n = len(pairs)
            for i, (ksl, qsl) in enumerate(pairs):
                S.add("pe", lambda: nc.tensor.matmul(ps[b][:, i * 128:(i + 1) * 128], lhsT=kn[:, ksl], rhs=qn[:, qsl], start=True, stop=True),
                      reads=[rkn, rqn], writes=[rps[b]])
            tmp, rtmp = tmps.next()
            bt, rbt = HD[h]["B4"][bkey]
            S.add("dve", lambda: nc.vector.tensor_tensor(out=tmp[:, 0:n * 128], in0=ps[b][:, 0:n * 128], in1=bt[:, 0:n * 128], op=ALU.add),
                  reads=[rps[b], rbt], writes=[rtmp])
            S.add("act", lambda: nc.scalar.activation(out=dst, in_=tmp[:, 0:n * 128], func=AF.Exp), reads=[rtmp], writes=[rdst])

        def p3A(u):
            h = u["h"]
            pt3, _r = pt3s.next()
            rpt3 = QR[id(pt3)]
            HD[h]["pt3"] = (pt3, rpt3)
            yh, _r = yhs.next()
            HD[h]["yh"] = (yh, QR[id(yh)])
            for bi in range(4):
                pairs = [(slice(r, T, 16), slice(r, T, 16)) for r in range(bi * 4, bi * 4 + 4)]
                qk_batch(h, pairs, (2, "cur"), pt3[:, bi * 512:(bi + 1) * 512], rpt3[bi])

        def gA(u):
            h, j = u["h"], u["j"]
            (vt1, rv1), (vt2, rv2), (vt3, rv3) = HD[h]["vt"]
            pt3, rpt3 = HD[h]["pt3"]
            ptg, _r = ptgs.next()
            rq4 = QR[id(ptg)]
            blk = lambda n: slice(n * 128, (n + 1) * 128)
            sub = lambda n, r: slice(512 * n + r, 512 * (n + 1), 4)
            mm = []
            qk_batch(h, [(blk(4 * j + i), blk(4 * j + i)) for i in range(4)], (0, "cur"), ptg[:, 0:512], rq4[0])
            for i in range(4):
                mm.append((slice(i * 128, (i + 1) * 128), vt1[:, 4 * j + i, :], rv1, ptg[:, i * 128:(i + 1) * 128], rq4[0]))
            pv = [i for i in range(4) if 4 * j + i >= 1]
            qk_batch(h, [(blk(4 * j + i - 1), blk(4 * j + i)) for i in pv], (0, "prev"), ptg[:, 512:512 + len(pv) * 128], rq4[1])
            for ii, i in enumerate(pv):
                mm.append((slice(i * 128, (i + 1) * 128), vt1[:, 4 * j + i - 1, :], rv1, ptg[:, 512 + ii * 128:512 + (ii + 1) * 128], rq4[1]))
            qk_batch(h, [(sub(j, r), sub(j, r)) for r in range(4)], (1, "cur"), ptg[:, 1024:1536], rq4[2])
            for r in range(4):
                mm.append((slice(r, 512, 4), vt2[:, j * 4 + r, :], rv2, ptg[:, 1024 + r * 128:1024 + (r + 1) * 128], rq4[2]))
            if j >= 1:
                qk_batch(h, [(sub(j - 1, r), sub(j, r)) for r in range(4)], (1, "prev"), ptg[:, 1536:2048], rq4[3])
                for r in range(4):
                    mm.append((slice(r, 512, 4), vt2[:, (j - 1) * 4 + r, :], rv2, ptg[:, 1536 + r * 128:1536 + (r + 1) * 128], rq4[3]))
            for r in range(16):
                mm.append((slice(r, 512, 16), vt3[:, r, :], rv3, pt3[:, r * 128 + 32 * j:r * 128 + 32 * j + 32], rpt3[r // 4]))
            u["mm"] = mm

        gcnt = [0]

        def gB(u):
            mm = u["mm"]
            bn, bd = (4, 5) if gcnt[0] % 2 == 0 else (6, 7)
            gcnt[0] += 1
            u["banks"] = (bn, bd)
            for idx, (cols_, lv, rlv, rhs, rrhs) in enumerate(mm):
                st, sp_ = idx == 0, idx == len(mm) - 1
                S.add("pe", lambda: nc.tensor.matmul(ps[bn][:, cols_], lhsT=lv, rhs=rhs, start=st, stop=sp_, skip_group_check=True), reads=[rlv, rrhs], writes=[rps[bn]])
                S.add("pe", lambda: nc.tensor.matmul(ps[bd][:, cols_], lhsT=kb.ones_bf[:], rhs=rhs, start=st, stop=sp_, skip_group_check=True),
                      reads=[kb.r_ones, rrhs], writes=[rps[bd]])

        def gC(u):
            h, j = u["h"], u["j"]
            bn, bd = u["banks"]
            yh, ryh = HD[h]["yh"]
            rd, rrd = rds.next()
            S.add("act", lambda: nc.scalar.activation(out=rd[:], in_=ps[bd][:], func=AF.Ln), reads=[rps[bd]], writes=[rrd])
            S.add("act", lambda: nc.scalar.activation(out=rd[:], in_=rd[:], func=AF.Exp, scale=-1.0), reads=[rrd], writes=[rrd])
            S.add("dve", lambda: nc.vector.tensor_tensor(out=yh[:, j * 512:(j + 1) * 512], in0=ps[bn][:], in1=rd[:], op=ALU.mult), reads=[rps[bn], rrd], writes=[ryh[j]])
            if j == 3:
                S.add("sp", lambda: nc.sync.dma_start(out=Y[(8 + h) * 128:(9 + h) * 128, :], in_=yh[:]), reads=ryh, dma=True)

        nop = lambda u: None
        units = []
        units.append(dict(h=0, st=(prep, nop, nop)))
        for h in range(8):
            units.append(dict(h=h, st=(p3A, nop, nop)))
            if h + 1 < 8:
                units.append(dict(h=h + 1, st=(prep, nop, nop)))
            for j in range(4):
                units.append(dict(h=h, j=j, st=(gA, gB, gC)))
        skewed(units, [lambda u: u["st"][0](u), lambda u: u["st"][1](u), lambda u: u["st"][2](u)])
        kb.end()


def t5_sel_table():
    pats = ((128, 1), (512, 4), (2048, 16))
    out = np.zeros((33, 3 * 384), np.float32)
    for p, (win, dil) in enumerate(pats):
        blk = win // dil
        for idx in range(384):
            rel = idx - 127
            if 0 <= rel <= blk and idx < 383:
                dist = np.int32(rel * dil)
                d_f = np.float32(max(int(dist), 1))
                large = 16 + np.int32(np.float32(np.log(np.float32(d_f / np.float32(16.0)))) / np.float32(np.log(2048.0 / 16.0)) * np.float32(16.0))
                large = min(int(large), 31)
                bkt = int(dist) if dist < 16 else large
                out[bkt, p * 384 + idx] = 1.0
            else:
                out[32, p * 384 + idx] = NEG
    return out


def cols(v, p=128):
    v = np.asarray(v, np.float32)
    n = v.shape[-1] // p
    a = v.reshape(-1, n, p)
    return np.ascontiguousarray(a.transpose(2, 0, 1).reshape(p, -1))


SM = {}


def build_smalls(inp):
    parts = []
    off = [0]

    def put(name, arr):
        arr = np.asarray(arr, np.float32)
        if arr.shape[0] < 128:
            arr = np.concatenate([arr, np.zeros((128 - arr.shape[0], arr.shape[1]), np.float32)], axis=0)
        SM[name] = (off[0], arr.shape[1])
        off[0] += arr.shape[1]
        parts.append(arr)
    put("ng", cols(inp["norm_g"]))
    cw = np.asarray(inp["ab_conv_w"], np.float32)[0]
    put("cw", np.ascontiguousarray(cw.reshape(4, 8, 128).transpose(2, 1, 0).reshape(128, 32)))
    put("cb", cols(inp["ab_conv_b"][0]))
    put("br", cols(inp["rg_b_r"][0]))
    put("bi", cols(inp["rg_b_i"][0]))
    put("lam", cols(inp["rg_lambda"][0]))
    put("gqk", np.ascontiguousarray(np.asarray(inp["qk_gain"], np.float32)[0].T))
    cw2 = np.asarray(inp["cd_conv_w"], np.float32)[0]
    put("cw2", np.ascontiguousarray(cw2.reshape(4, 16, 128).transpose(2, 1, 0).reshape(128, 64)))
    put("cb2", cols(inp["cd_conv_b"][0]))
    put("gb", np.ascontiguousarray(np.asarray(inp["mlstm_gate_bias"], np.float32)[0].T))
    put("hg", cols(inp["mlstm_h_gain"][0]))
    return np.ascontiguousarray(np.concatenate(parts, axis=1))


PHASE_GROUPS = {"ab": ("abproj", "abrg", "abattn", "about"), "cd": ("cdproj", "cdml", "cdsb", "cdout")}
ALL_PHASES = ("ffn00", "ab", "ffn01", "ffn10", "cd", "ffn11")
DEBUG_OUT = None


def build_program(nsm, phases):
    nc = bass.Bass("TRN2", target_bir_lowering=False)
    need_ffn = any(p.startswith("ffn") for p in phases)
    need_ab = "ab" in phases
    need_cd = "cd" in phases
    d = {}

    def din(name, shape):
        d[name] = nc.dram_tensor(name, list(shape), F32, kind="ExternalInput").ap()
    din("xT", [DM, SEQ])
    din("smalls", [128, nsm])
    if need_ffn:
        din("ffn_w_in", [2, 2, DM, 2 * DFF])
        din("ffn_w_out", [2, 2, DFF, DM])
    if need_ab:
        din("ab_w_in", [1, DM, 5120]); din("rg_w_r", [1, 8, 128, 128]); din("rg_w_i", [1, 8, 128, 128])
        din("rel_bias", [32, 8]); din("selT", [33, 1152]); din("ab_w_out", [1, DM, DM])
    if need_cd:
        din("cd_w_in", [1, DM, 7176]); din("cd_w_out", [1, DM, DM]); din("cdc", [128, CDC_N])
    outT = nc.dram_tensor("outT", [DM, SEQ], F32, kind="ExternalOutput").ap()

    def scratch(name, shape, dt):
        kind = "ExternalOutput" if DEBUG_OUT == name else "Internal"
        return nc.dram_tensor(name, list(shape), dt, kind=kind).ap()
    xa = scratch("xa", [DM, SEQ], F32)
    xb = scratch("xb", [DM, SEQ], F32)
    PROJ = scratch("PROJ", [4096, SEQ], F32)
    VT = scratch("VT", [SEQ, 2048], BF16)
    Y = scratch("Y", [DM, SEQ], BF16)
    ZT = scratch("ZT", [8, 128, 1152], F32)
    QKD = scratch("QKD", [2048, SEQ], BF16)
    GIF = scratch("GIF", [8, SEQ], F32)
    BROW = scratch("BROW", [4, SEQ], F32)
    C1COL = scratch("C1COL", [128, 64], F32)
    RSTD = scratch("RSTD", [128, SEQ], F32)
    with ExitStack() as es:
        kb = KB(nc, es)
        S = kb.S
        rnc = kb.nc
        smalls = es.enter_context(nc.sbuf_tensor("smalls_sb", [128, nsm], F32))
        ones_bf = es.enter_context(nc.sbuf_tensor("ones_bf", [128, 128], BF16))
        kb.smalls, kb.ones_bf = smalls, ones_bf
        kb.r_smalls, kb.r_ones = Res(), Res()
        kb.begin()
        S.add("sp", lambda: rnc.sync.dma_start(out=smalls[:], in_=d["smalls"]), writes=[kb.r_smalls], dma=True)
        S.add("dve", lambda: rnc.vector.memset(ones_bf[:], 1.0), writes=[kb.r_ones])
        kb.end()
        kb.r_smalls, kb.r_ones = Res(), Res()

        def ng(l, i):
            o = SM["ng"][0] + (l * 3 + i) * 16
            return smalls[:, o:o + 16]

        cur, curkey = d["xT"], "x0"
        bufs = [xa, xb]
        def next_gain(pi):
            if pi + 1 >= len(phases):
                return None
            nx = phases[pi + 1]
            if nx.startswith("ffn"):
                return ng(int(nx[3]), 0 if int(nx[4]) == 0 else 2)
            return ng(0, 1) if nx == "ab" else ng(1, 1)

        for pi, ph in enumerate(phases):
            last = pi == len(phases) - 1
            dst = outT if last else bufs[pi % 2]
            dkey = f"x{pi + 1}"
            gn = next_gain(pi)
            if ph.startswith("ffn"):
                l, i = int(ph[3]), int(ph[4])
                ffn_phase(kb, cur, curkey, dst, dkey, ng(l, 0 if i == 0 else 2), d["ffn_w_in"][l, i], d["ffn_w_out"][l, i], RSTD, g_next=gn)
            elif ph == "ab":
                fm = [(c * 128, 128, PROJ[c * 128:(c + 1) * 128, :], F32, 1.0) for c in range(32)]
                tm = [(4096 + g * 512, VT[:, g * 512:(g + 1) * 512]) for g in range(2)]
                proj_phase(kb, cur, curkey, ng(0, 1), d["ab_w_in"][0], fm, tm, RSTD)
                abrg_phase(kb, PROJ, Y, d)
                abattn_phase(kb, PROJ, VT, Y, ZT, d)
                outproj_phase(kb, cur, curkey, dst, dkey, Y, d["ab_w_out"][0], RSTD, g_next=gn)
            elif ph == "cd":
                cd_phases(kb, cur, curkey, dst, dkey, ng(1, 1), d, PROJ, VT, Y, QKD, GIF, BROW, C1COL, RSTD, gn)
            else:
                raise ValueError(ph)
            cur, curkey = dst, dkey
        x_close(kb)
        print("ops", S.tot_ops, "waits", S.tot_waits, "sig", S.ccount)
    return nc


CDC_N = 1412
CD_SUB = ("gate", "ml", "sb")
C_ID4, C_SEL, C_M4T, C_TRIL, C_NEGSB, C_IDENT = 0, 4, 516, 1028, 1156, 1284


def cd_consts():
    c = np.zeros((128, CDC_N), np.float32)
    c[0:4, 0:4] = np.eye(4, dtype=np.float32)
    for hh in range(4):
        c[hh, C_SEL + hh * 128:C_SEL + (hh + 1) * 128] = 1.0
    s_ = np.arange(128)[:, None]
    l_ = np.arange(128)[None, :]
    m = np.where(s_ <= l_, 0.0, NEG).astype(np.float32)
    for rep in range(4):
        c[:, C_M4T + rep * 128:C_M4T + (rep + 1) * 128] = m
    c[:, C_TRIL:C_TRIL + 128] = (l_ < s_).astype(np.float32)
    c[:, C_NEGSB:C_NEGSB + 128] = np.where(l_ < s_, 0.0, NEG)
    c[:, C_IDENT:C_IDENT + 128] = np.eye(128, dtype=np.float32)
    return c


def cdgate_phase(kb, GIF, BROW, C1COL, d):
    nc, S = kb.nc, kb.S
    T = SEQ
    sm = kb.smalls
    with ExitStack() as es:
        kb.begin()
        rs_ = kb.r_smalls
        ps0 = es.enter_context(nc.psum_tensor("psg", [128, 512], F32)); rps0 = Res()
        fb = es.enter_context(nc.sbuf_tensor("fb", [4, T], F32)); rfb = Res()
        ib = es.enter_context(nc.sbuf_tensor("ib", [4, T], F32)); rib = Res()
        rm = es.enter_context(nc.sbuf_tensor("rm", [4, T], F32)); rrm = Res()
        bb = es.enter_context(nc.sbuf_tensor("bb", [4, T], F32)); rbb = Res()
        id4 = es.enter_context(nc.sbuf_tensor("id4", [4, 4], F32)); rid = Res()
        c1 = es.enter_context(nc.sbuf_tensor("c1", [128, 64], F32)); rc1 = Res()
        go = SM["gb"][0]
        S.add("sp", lambda: nc.sync.dma_start(out=fb[:], in_=GIF[4:8, :]), writes=[rfb], dma=True)
        S.add("sp", lambda: nc.sync.dma_start(out=ib[:], in_=GIF[0:4, :]), writes=[rib], dma=True)
        S.add("sp", lambda: nc.sync.dma_start(out=id4[:], in_=d["cdc"][0:4, C_ID4:C_ID4 + 4]), writes=[rid], dma=True)
        S.add("dve", lambda: nc.vector.tensor_scalar(out=fb[:], in0=fb[:], scalar1=sm[0:4, go + 1:go + 2], scalar2=None, op0=ALU.add), reads=[rfb, rs_], writes=[rfb])
        S.add("act", lambda: nc.scalar.activation(out=fb[:], in_=fb[:], func=AF.Exp, scale=-1.0), reads=[rfb], writes=[rfb])
        S.add("act", lambda: nc.scalar.activation(out=fb[:], in_=fb[:], func=AF.Ln, bias=1.0, scale=1.0), reads=[rfb], writes=[rfb])
        S.add("dve", lambda: nc.vector.tensor_scalar(out=fb[:], in0=fb[:], scalar1=-1.0, scalar2=None, op0=ALU.mult), reads=[rfb], writes=[rfb])
        S.add("dve", lambda: nc.vector.memset(rm[:], 1.0), writes=[rrm])
        S.add("dve", lambda: nc.vector.memset(rm[:, 0:T:128], 0.0), reads=[rrm], writes=[rrm])
        S.add("dve", lambda: nc.vector.tensor_tensor_scan(out=bb[:], data0=rm[:], data1=fb[:], initial=0.0, op0=ALU.mult, op1=ALU.add), reads=[rrm, rfb], writes=[rbb])
        S.add("dve", lambda: nc.vector.tensor_scalar(out=ib[:], in0=ib[:], scalar1=sm[0:4, go:go + 1], scalar2=None, op0=ALU.add), reads=[rib, rs_], writes=[rib])
        S.add("dve", lambda: nc.vector.tensor_tensor(out=ib[:], in0=ib[:], in1=bb[:], op=ALU.subtract), reads=[rib, rbb], writes=[rib])
        for n in range(16):
            S.add("pe", lambda n=n: nc.tensor.matmul(ps0[:, n * 4:(n + 1) * 4], lhsT=ib[0:4, n * 128:(n + 1) * 128], rhs=id4[0:4, 0:4], start=True, stop=True),
                  reads=[rib, rid], writes=[rps0])
        S.add("act", lambda: nc.scalar.copy(out=c1[:], in_=ps0[:, 0:64]), reads=[rps0], writes=[rc1])
        S.add("sp", lambda: nc.sync.dma_start(out=C1COL, in_=c1[:]), reads=[rc1], dma=True)
        S.add("sp", lambda: nc.sync.dma_start(out=BROW, in_=bb[:]), reads=[rbb], dma=True)
        kb.end()


def cdml_phase(kb, PROJ, VT, Y, BROW, C1COL, d):
    nc, S = kb.nc, kb.S
    T = SEQ
    sm = kb.smalls
    def smc(name, j):
        o = SM[name][0] + j
        return sm[:, o:o + 1]
    with ExitStack() as es:
        kb.begin()
        rs_ = kb.r_smalls
        ps = [es.enter_context(nc.psum_tensor(f"pm{i}", [128, 512], F32)) for i in range(7)]
        rps = [Res() for _ in range(7)]
        pT = es.enter_context(nc.psum_tensor("pT", [128, 1024], BF16)); rpT = Res()
        cdc = es.enter_context(nc.sbuf_tensor("cdc", [128, CDC_N], F32)); rcdc = Res()
        S.add("sp", lambda: nc.sync.dma_start(out=cdc[:], in_=d["cdc"]), writes=[rcdc], dma=True)
        identb = es.enter_context(nc.sbuf_tensor("identb", [128, 128], BF16)); rident = Res()
        S.add("dve", lambda: nc.vector.tensor_copy(out=identb[:], in_=cdc[:, C_IDENT:C_IDENT + 128]), reads=[rcdc], writes=[rident])
        brow = es.enter_context(nc.sbuf_tensor("brow", [4, T], F32)); rbrow = Res()
        S.add("sp", lambda: nc.sync.dma_start(out=brow[:], in_=BROW), writes=[rbrow], dma=True)
        c1c = es.enter_context(nc.sbuf_tensor("c1c", [128, 64], F32)); rc1c = Res()
        S.add("sp", lambda: nc.sync.dma_start(out=c1c[:], in_=C1COL), writes=[rc1c], dma=True)
        xps = Slots(nc, es, "xp", [128, T + 4], F32, 2)
        for i in range(2):
            S.add("dve", lambda i=i: nc.vector.memset(xps.t[i][:, 0:4], 0.0), writes=[xps.r[i]])
        xcs = Slots(nc, es, "xc", [128, T], F32, 2)
        qkb = [es.enter_context(nc.sbuf_tensor(f"qkb{i}", [128, T], BF16)) for i in range(4)]; rqkb = [Res() for _ in range(4)]
        qs = [es.enter_context(nc.sbuf_tensor(f"qs{i}", [128, T], BF16)) for i in range(2)]; rqs = [Res() for _ in range(2)]
        osg = [es.enter_context(nc.sbuf_tensor(f"osg{i}", [128, T], F32)) for i in range(2)]; rosg = [Res() for _ in range(2)]
        vx = es.enter_context(nc.sbuf_tensor("vx", [128, 16, 384], BF16)); rvx = Res()
        S.add("dve", lambda: nc.vector.memset(vx[:, :, 256:384], 1.0), writes=[rvx])
        eb = es.enter_context(nc.sbuf_tensor("eb", [128, T], F32))
        bm = es.enter_context(nc.sbuf_tensor("bm", [128, T], F32))
        st = es.enter_context(nc.sbuf_tensor("st", [128, T], BF16))
        wk = es.enter_context(nc.sbuf_tensor("wk", [128, 16, 256], BF16))
        hc = [es.enter_context(nc.sbuf_tensor(f"hc{i}", [128, T], F32)) for i in range(2)]
        wg = es.enter_context(nc.sbuf_tensor("wg", [128, 16], F32)); rwg = Res()
        ebl = es.enter_context(nc.sbuf_tensor("ebl", [128, 16], F32)); rebl = Res()
        cst = [es.enter_context(nc.sbuf_tensor(f"cst{i}", [128, 384], F32)) for i in range(2)]
        cbf = es.enter_context(nc.sbuf_tensor("cbf", [128, 30, 384], BF16))
        t1s = Slots(nc, es, "t1", [128, 128], F32, 4)
        yos = Slots(nc, es, "yo", [128, T], BF16, 2)
        reb = [Res() for _ in range(4)]
        rbm = [Res() for _ in range(4)]
        rst = [Res() for _ in range(4)]
        rwk = [Res() for _ in range(16)]
        rhc = [[Res() for _ in range(16)] for _ in range(2)]
        rcst = [Res(), Res()]
        rcbf = [[Res(), Res()] for _ in range(15)]
        for hh in range(4):
            for w_ in range(2):
                for dkc in range(2):
                    cc = w_ * 8 + hh * 2 + dkc
                    xp, rxp = xps.next(); xc, rxc = xcs.next()
                    S.add("sp", lambda: nc.sync.dma_start(out=xp[:, 4:4 + T], in_=PROJ[cc * 128:(cc + 1) * 128, :]), reads=[rxp], writes=[rxp], dma=True)
                    S.add("dve", lambda: nc.vector.tensor_scalar(out=xc[:], in0=xp[:, 4:4 + T], scalar1=smc("cw2", cc * 4 + 3), scalar2=smc("cb2", cc),
                                                                 op0=ALU.mult, op1=ALU.add), reads=[rxp, rs_], writes=[rxc])
                    for j in range(3):
                        S.add("dve", lambda: nc.vector.scalar_tensor_tensor(out=xc[:], in0=xp[:, 1 + j:1 + j + T], scalar=smc("cw2", cc * 4 + j), in1=xc[:],
                                                                            op0=ALU.mult, op1=ALU.add), reads=[rxp, rs_, rxc], writes=[rxc])
                    qi = w_ * 2 + dkc
                    S.add("act", lambda: nc.scalar.activation(out=qkb[qi][:], in_=xc[:], func=AF.Silu), reads=[rxc], writes=[rqkb[qi]])
            qb, kbf = qkb[0:2], qkb[2:4]
            rqb, rkb = rqkb[0:2], rqkb[2:4]
            for dvc in range(2):
                S.add("sp", lambda: nc.sync.dma_start(out=osg[dvc][:], in_=PROJ[2048 + hh * 256 + dvc * 128:2048 + hh * 256 + (dvc + 1) * 128, :]), writes=[rosg[dvc]], dma=True)
                S.add("act", lambda: nc.scalar.activation(out=osg[dvc][:], in_=osg[dvc][:], func=AF.Sigmoid), reads=[rosg[dvc]], writes=[rosg[dvc]])
            S.add("sp", lambda: nc.sync.dma_start(out=vx[:, :, 0:256], in_=VT.rearrange("(n i) f -> i n f", i=128)[:, :, hh * 256:(hh + 1) * 256]), reads=[rvx], writes=[rvx], dma=True)
            for tt in range(4):
                sl = slice(tt * 512, (tt + 1) * 512)
                S.add("pe", lambda: nc.tensor.matmul(ps[tt][:], lhsT=cdc[0:4, C_SEL + hh * 128:C_SEL + (hh + 1) * 128], rhs=brow[0:4, sl], start=True, stop=True),
                      reads=[rcdc, rbrow], writes=[rps[tt]])
                S.add("act", lambda: nc.scalar.activation(out=eb[:, sl], in_=ps[tt][:], func=AF.Exp), reads=[rps[tt]], writes=[reb[tt]])
                S.add("dve", lambda: nc.vector.tensor_tensor(out=bm[:, sl], in0=ps[tt][:], in1=cdc[:, C_M4T:C_M4T + 512], op=ALU.add), reads=[rps[tt], rcdc], writes=[rbm[tt]])
            S.add("dve", lambda: nc.vector.tensor_copy(out=ebl[:], in_=eb[:, 127:T:128]), reads=reb, writes=[rebl])
            S.add("dve", lambda: nc.vector.tensor_tensor(out=wg[:], in0=c1c[:, hh:64:4], in1=bm[:, 127:T:128], op=ALU.add), reads=[rc1c] + rbm, writes=[rwg])
            S.add("act", lambda: nc.scalar.activation(out=wg[:], in_=wg[:], func=AF.Exp), reads=[rwg], writes=[rwg])
            for dkc in range(2):
                S.add("dve", lambda: nc.vector.tensor_tensor(out=qs[dkc][:], in0=qb[dkc][:], in1=eb[:], op=ALU.mult), reads=[rqb[dkc]] + reb, writes=[rqs[dkc]])
            for g in range(4):
                for i in range(4):
                    n = g * 4 + i
                    sl = slice(n * 128, (n + 1) * 128)
                    S.add("act", lambda: nc.scalar.activation(out=bm[:, sl], in_=bm[:, sl], func=AF.Exp, bias=c1c[:, n * 4 + hh:n * 4 + hh + 1], scale=1.0),
                          reads=[rbm[g], rc1c], writes=[rbm[g]])
                    for dkc in range(2):
                        S.add("pe", lambda: nc.tensor.matmul(ps[g][:, i * 128:(i + 1) * 128], lhsT=kbf[dkc][:, sl], rhs=qb[dkc][:, sl], start=(dkc == 0), stop=(dkc == 1)),
                              reads=[rkb[dkc], rqb[dkc]], writes=[rps[g]])
                gs = slice(g * 512, (g + 1) * 512)
                S.add("dve", lambda: nc.vector.scalar_tensor_tensor(out=st[:, gs], in0=ps[g][:], scalar=0.0625, in1=bm[:, gs], op0=ALU.mult, op1=ALU.mult),
                      reads=[rps[g], rbm[g]], writes=[rst[g]])
            for g in range(4):
                for i in range(4):
                    n = g * 4 + i
                    for dkc in range(2):
                        S.add("pe", lambda: nc.tensor.transpose(out=pT[:, i * 256 + dkc * 128:i * 256 + (dkc + 1) * 128], in_=kbf[dkc][:, n * 128:(n + 1) * 128], identity=identb[:]),
                              reads=[rkb[dkc], rident], writes=[rpT])
                for i in range(4):
                    n = g * 4 + i
                    S.add("dve", lambda: nc.vector.tensor_scalar(out=wk[:, n, :], in0=pT[:, i * 256:(i + 1) * 256], scalar1=wg[:, n:n + 1], scalar2=0.0625,
                                                                 op0=ALU.mult, op1=ALU.mult), reads=[rpT, rwg], writes=[rwk[n]])
            for dkc in range(2):
                S.add("dve", lambda: nc.vector.memset(cst[dkc][:], 0.0), writes=[rcst[dkc]])
            cl = 0
            for n in range(15):
                for dkc in range(2):
                    b = 4 + (cl % 3)
                    cl += 1
                    S.add("pe", lambda: nc.tensor.matmul(ps[b][:, 0:384], lhsT=wk[:, n, dkc * 128:(dkc + 1) * 128], rhs=vx[:, n, :], start=True, stop=True),
                          reads=[rwk[n], rvx], writes=[rps[b]])
                    S.add("dve", lambda: nc.vector.scalar_tensor_tensor(out=cst[dkc][:], in0=cst[dkc][:], scalar=ebl[:, n:n + 1], in1=ps[b][:, 0:384], op0=ALU.mult, op1=ALU.add),
                          reads=[rcst[dkc], rebl, rps[b]], writes=[rcst[dkc]])
                    S.add("act", lambda: nc.scalar.copy(out=cbf[:, n * 2 + dkc, :], in_=cst[dkc][:]), reads=[rcst[dkc]], writes=[rcbf[n][dkc]])
            for n in range(16):
                sl = slice(n * 128, (n + 1) * 128)
                g = n // 4
                bnd = 4 + (n % 3)
                mm = [(slice(0, 128), vx[:, n, 0:128], rvx, st[:, sl], rst[g]),
                      (slice(128, 256), vx[:, n, 128:256], rvx, st[:, sl], rst[g]),
                      (slice(256, 384), kb.ones_bf[:], kb.r_ones, st[:, sl], rst[g])]
                if n > 0:
                    for dkc in range(2):
                        for cs in range(3):
                            mm.append((slice(cs * 128, (cs + 1) * 128), cbf[:, (n - 1) * 2 + dkc, cs * 128:(cs + 1) * 128], rcbf[n - 1][dkc], qs[dkc][:, sl], rqs[dkc]))
                for idx, (cols_, lv, rlv, rhs, rrhs) in enumerate(mm):
                    S.add("pe", lambda: nc.tensor.matmul(ps[bnd][:, cols_], lhsT=lv, rhs=rhs, start=(idx == 0), stop=(idx == len(mm) - 1), skip_group_check=True),
                          reads=[rlv, rrhs], writes=[rps[bnd]])
                t1, rt1 = t1s.next()
                S.add("act", lambda: nc.scalar.activation(out=t1[:], in_=ps[bnd][:, 256:384], func=AF.Abs), reads=[rps[bnd]], writes=[rt1])
                S.add("dve", lambda: nc.vector.tensor_scalar(out=t1[:], in0=t1[:], scalar1=1.0, scalar2=None, op0=ALU.max), reads=[rt1], writes=[rt1])
                S.add("dve", lambda: nc.vector.reciprocal(out=t1[:], in_=t1[:]), reads=[rt1], writes=[rt1])
                for dvc in range(2):
                    S.add("dve", lambda: nc.vector.tensor_tensor(out=hc[dvc][:, sl], in0=ps[bnd][:, dvc * 128:(dvc + 1) * 128], in1=t1[:], op=ALU.mult),
                          reads=[rps[bnd], rt1], writes=[rhc[dvc][n]])
            for dvc in range(2):
                S.add("act", lambda: nc.scalar.activation(out=st[:], in_=hc[dvc][:], func=AF.Square), reads=rhc[dvc], writes=rst)
                for tt in range(4):
                    S.add("pe", lambda: nc.tensor.matmul(ps[tt][:], lhsT=kb.ones_bf[:], rhs=st[:, tt * 512:(tt + 1) * 512], start=(dvc == 0), stop=(dvc == 1)),
                          reads=rst + [kb.r_ones], writes=[rps[tt]])
            for tt in range(4):
                S.add("act", lambda: nc.scalar.activation(out=bm[:, tt * 512:(tt + 1) * 512], in_=ps[tt][:], func=AF.Ln, scale=1.0 / 256, bias=EPS), reads=[rps[tt]], writes=[rbm[tt]])
            S.add("act", lambda: nc.scalar.activation(out=bm[:], in_=bm[:], func=AF.Exp, scale=-0.5), reads=rbm, writes=rbm)
            for dvc in range(2):
                S.add("dve", lambda: nc.vector.scalar_tensor_tensor(out=eb[:], in0=hc[dvc][:], scalar=smc("hg", hh * 2 + dvc), in1=bm[:], op0=ALU.mult, op1=ALU.mult),
                      reads=rhc[dvc] + rbm + [rs_], writes=reb)
                yo, ryo = yos.next()
                S.add("dve", lambda: nc.vector.tensor_tensor(out=yo[:], in0=eb[:], in1=osg[dvc][:], op=ALU.mult), reads=reb + [rosg[dvc]], writes=[ryo])
                S.add("sp", lambda: nc.sync.dma_start(out=Y[hh * 256 + dvc * 128:hh * 256 + (dvc + 1) * 128, :], in_=yo[:]), reads=[ryo], dma=True)
        kb.end()


def skewed(units, stages):
    n, k = len(units), len(stages)
    for t in range(n + k - 1):
        for s_ in reversed(range(k)):
            u = t - s_
            if 0 <= u < n:
                stages[s_](units[u])


def cdsb_phase(kb, QKD, VT, Y, d):
    nc, S = kb.nc, kb.S
    T = SEQ
    with ExitStack() as es:
        kb.begin()
        ps = [es.enter_context(nc.psum_tensor(f"pz{i}", [128, 512], F32)) for i in range(6)]
        rps = [Res() for _ in range(6)]
        pTs = [es.enter_context(nc.psum_tensor(f"pTs{i}", [128, 1024], BF16)) for i in range(2)]
        rpT = [Res(), Res()]
        cdc = es.enter_context(nc.sbuf_tensor("cdc", [128, CDC_N], F32)); rcdc = Res()
        S.add("sp", lambda: nc.sync.dma_start(out=cdc[:], in_=d["cdc"]), writes=[rcdc], dma=True)
        identb = es.enter_context(nc.sbuf_tensor("identb", [128, 128], BF16)); rident = Res()
        S.add("dve", lambda: nc.vector.tensor_copy(out=identb[:], in_=cdc[:, C_IDENT:C_IDENT + 128]), reads=[rcdc], writes=[rident])
        onesf = es.enter_context(nc.sbuf_tensor("onesf", [128, T], F32)); rof = Res()
        S.add("dve", lambda: nc.vector.memset(onesf[:], 1.0), writes=[rof])
        qks = Slots(nc, es, "qk", [128, T], BF16, 4)
        vts = Slots(nc, es, "vt", [128, 16, 128], BF16, 2)
        sps = Slots(nc, es, "sp", [128, T + 1], F32, 4)
        fs = Slots(nc, es, "F", [128, T], F32, 4)
        ats = Slots(nc, es, "at", [128, T], BF16, 3)
        aTs = Slots(nc, es, "aT", [128, T], BF16, 3)
        nts = Slots(nc, es, "nt", [128, 1], F32, 6)
        yds = Slots(nc, es, "yd", [128, T], BF16, 2)
        for i in range(4):
            S.add("dve", lambda i=i: nc.vector.memset(sps.t[i][:, 0:1], 0.0), writes=[sps.r[i]])
        zcnt = [0]
        z2cnt = [0]
        tcnt = [0]
        heads = {}

        def s0(u):
            h, n = u["h"], u["n"]
            if n == 0:
                q, rq = qks.next(); k, rk = qks.next()
                S.add("sp", lambda: nc.sync.dma_start(out=q[:], in_=QKD[h * 128:(h + 1) * 128, :]), writes=[rq], dma=True)
                S.add("sp", lambda: nc.sync.dma_start(out=k[:], in_=QKD[1024 + h * 128:1024 + (h + 1) * 128, :]), writes=[rk], dma=True)
                vt, rvt = vts.next()
                S.add("sp", lambda: nc.sync.dma_start(out=vt[:], in_=VT.rearrange("(b i) f -> i b f", i=128)[:, :, 1024 + h * 128:1024 + (h + 1) * 128]), writes=[rvt], dma=True)
                yd, ryd = yds.next()
                heads[h] = (q, rq, k, rk, vt, rvt, yd, ryd)
            q, rq, k, rk, vt, rvt, yd, ryd = heads[h]
            Kn = (n + 1) * 128
            nb = (Kn + 511) // 512
            sp_, rsp = sps.next()
            u.update(Kn=Kn, nb=nb, sp=sp_, rsp=rsp)
            for bi in range(nb):
                w_ = min(512, Kn - bi * 512)
                ksl = slice(bi * 512, bi * 512 + w_)
                b = zcnt[0] % 2
                zcnt[0] += 1
                S.add("pe", lambda: nc.tensor.matmul(ps[b][:, 0:w_], lhsT=q[:, n * 128:(n + 1) * 128], rhs=k[:, ksl], start=True, stop=True),
                      reads=[rq, rk], writes=[rps[b]])
                S.add("act", lambda: nc.scalar.activation(out=sp_[:, 1 + bi * 512:1 + bi * 512 + w_], in_=ps[b][:, 0:w_], func=AF.Exp), reads=[rps[b]], writes=[rsp])

        def s0b(u):
            n, Kn, sp_, rsp = u["n"], u["Kn"], u["sp"], u["rsp"]
            S.add("act", lambda: nc.scalar.activation(out=sp_[:, 1:1 + Kn], in_=sp_[:, 1:1 + Kn], func=AF.Ln, bias=1.0, scale=1.0), reads=[rsp], writes=[rsp])
            S.add("dve", lambda: nc.vector.tensor_tensor(out=sp_[:, 1 + n * 128:1 + Kn], in0=sp_[:, 1 + n * 128:1 + Kn], in1=cdc[:, C_TRIL:C_TRIL + 128], op=ALU.mult),
                  reads=[rsp, rcdc], writes=[rsp])

        def s1a(u):
            n, Kn, sp_, rsp = u["n"], u["Kn"], u["sp"], u["rsp"]
            F, rF = fs.next()
            u.update(F=F, rF=rF)
            S.add("dve", lambda: nc.vector.tensor_tensor_scan(out=F[:, 0:Kn], data0=onesf[:, 0:Kn], data1=sp_[:, 0:Kn], initial=0.0, op0=ALU.mult, op1=ALU.add),
                  reads=[rof, rsp], writes=[rF])

        def s1(u):
            h, n, Kn, nb, F, rF = u["h"], u["n"], u["Kn"], u["nb"], u["F"], u["rF"]
            q, rq, k, rk, vt, rvt, yd, ryd = heads[h]
            nt, rnt = nts.next()
            u.update(nt=nt, rnt=rnt)
            S.add("dve", lambda: nc.vector.tensor_scalar(out=nt[:], in0=F[:, Kn - 1:Kn], scalar1=-1.0, scalar2=None, op0=ALU.mult), reads=[rF], writes=[rnt])
            for bi in range(nb):
                w_ = min(512, Kn - bi * 512)
                ksl = slice(bi * 512, bi * 512 + w_)
                b = 2 + (z2cnt[0] % 2)
                z2cnt[0] += 1
                S.add("pe", lambda: nc.tensor.matmul(ps[b][:, 0:w_], lhsT=q[:, n * 128:(n + 1) * 128], rhs=k[:, ksl], start=True, stop=True),
                      reads=[rq, rk], writes=[rps[b]])
                S.add("dve", lambda: nc.vector.tensor_tensor(out=F[:, ksl], in0=ps[b][:, 0:w_], in1=F[:, ksl], op=ALU.add), reads=[rps[b], rF, rnt], writes=[rF])
            S.add("dve", lambda: nc.vector.tensor_tensor(out=F[:, n * 128:Kn], in0=F[:, n * 128:Kn], in1=cdc[:, C_NEGSB:C_NEGSB + 128], op=ALU.add),
                  reads=[rF, rcdc], writes=[rF])

        def s2(u):
            Kn, F, rF, nt, rnt = u["Kn"], u["F"], u["rF"], u["nt"], u["rnt"]
            at, rat = ats.next()
            u.update(at=at, rat=rat)
            S.add("act", lambda: nc.scalar.activation(out=at[:, 0:Kn], in_=F[:, 0:Kn], func=AF.Exp, bias=nt[:, 0:1], scale=1.0), reads=[rF, rnt], writes=[rat])

        def s3(u):
            n, at, rat = u["n"], u["at"], u["rat"]
            aT, raT = aTs.next()
            u.update(aT=aT, raT=raT)
            for g in range((n + 8) // 8):
                tb = tcnt[0] % 2
                tcnt[0] += 1
                k0, k1 = g * 8, min(n + 1, g * 8 + 8)
                for kk in range(k0, k1):
                    S.add("pe", lambda: nc.tensor.transpose(out=pTs[tb][:, (kk - k0) * 128:(kk - k0 + 1) * 128], in_=at[:, kk * 128:(kk + 1) * 128],
                                                            identity=identb[:]), reads=[rat, rident], writes=[rpT[tb]])
                wdt = (k1 - k0) * 128
                if tb == 0:
                    S.add("act", lambda: nc.scalar.copy(out=aT[:, k0 * 128:k0 * 128 + wdt], in_=pTs[tb][:, 0:wdt]), reads=[rpT[tb]], writes=[raT])
                else:
                    S.add("dve", lambda: nc.vector.tensor_copy(out=aT[:, k0 * 128:k0 * 128 + wdt], in_=pTs[tb][:, 0:wdt]), reads=[rpT[tb]], writes=[raT])

        def s4(u):
            h, n, aT, raT = u["h"], u["n"], u["aT"], u["raT"]
            q, rq, k, rk, vt, rvt, yd, ryd = heads[h]
            ob = 4 + ((n // 4) % 2)
            oc = slice((n % 4) * 128, (n % 4 + 1) * 128)
            for kk in range(n + 1):
                S.add("pe", lambda: nc.tensor.matmul(ps[ob][:, oc], lhsT=vt[:, kk, :], rhs=aT[:, kk * 128:(kk + 1) * 128], start=(kk == 0), stop=(kk == n)),
                      reads=[rvt, raT], writes=[rps[ob]])
            if n % 4 == 3:
                g4 = n // 4
                S.add("dve", lambda: nc.vector.tensor_copy(out=yd[:, g4 * 512:(g4 + 1) * 512], in_=ps[ob][:]), reads=[rps[ob]], writes=[ryd])
            if n == 15:
                S.add("sp", lambda: nc.sync.dma_start(out=Y[1024 + h * 128:1024 + (h + 1) * 128, :], in_=yd[:]), reads=[ryd], dma=True)

        units = [dict(h=h, n=n) for h in range(8) for n in range(16)]
        skewed(units, [s0, s0b, s1a, s1, s2, s3, s4])
        kb.end()


def cd_phases(kb, cur, curkey, dst, dkey, gcol, d, PROJ, VT, Y, QKD, GIF, BROW, C1COL, RSTD, gn):
    w = d["cd_w_in"][0]
    fm = [(c * 128, 128, PROJ[c * 128:(c + 1) * 128, :], F32, 1.0) for c in range(16)]
    fm += [(3072 + c * 128, 128, PROJ[2048 + c * 128:2048 + (c + 1) * 128, :], F32, 1.0) for c in range(8)]
    fm += [(4096, 4, GIF[0:4, :], F32, 1.0), (4100, 4, GIF[4:8, :], F32, 1.0)]
    fm += [(4104 + c * 128, 128, QKD[c * 128:(c + 1) * 128, :], BF16, 128.0 ** -0.5) for c in range(8)]
    fm += [(5128 + c * 128, 128, QKD[1024 + c * 128:1024 + (c + 1) * 128, :], BF16, 1.0) for c in range(8)]
    tm = [(2048 + g * 512, VT[:, g * 512:(g + 1) * 512]) for g in range(2)]
    tm += [(6152 + g * 512, VT[:, 1024 + g * 512:1024 + (g + 1) * 512]) for g in range(2)]
    proj_phase(kb, cur, curkey, gcol, w, fm, tm, RSTD)
    if "gate" in CD_SUB:
        cdgate_phase(kb, GIF, BROW, C1COL, d)
    if "ml" in CD_SUB:
        cdml_phase(kb, PROJ, VT, Y, BROW, C1COL, d)
    if "sb" in CD_SUB:
        cdsb_phase(kb, QKD, VT, Y, d)
    outproj_phase(kb, cur, curkey, dst, dkey, Y, d["cd_w_out"][0], RSTD, g_next=gn)


def kernel(**inp):
    phases = tuple(ALL_PHASES)
    x = np.asarray(inp["x"], np.float32)
    sm = build_smalls(inp)
    nc = build_program(sm.shape[1], phases)
    shared = {"smalls": sm}
    f32c = lambda k: np.ascontiguousarray(np.asarray(inp[k], np.float32))
    if any(p.startswith("ffn") for p in phases):
        shared["ffn_w_in"] = f32c("ffn_w_in"); shared["ffn_w_out"] = f32c("ffn_w_out")
    if "ab" in phases:
        for k in ("ab_w_in", "rg_w_r", "rg_w_i", "rel_bias", "ab_w_out"):
            shared[k] = f32c(k)
        shared["selT"] = t5_sel_table()
    if "cd" in phases:
        for k in ("cd_w_in", "cd_w_out"):
            shared[k] = f32c(k)
        shared["cdc"] = cd_consts()
    in_maps = []
    for b in range(NCORES):
        m = dict(shared)
        m["xT"] = np.ascontiguousarray(x[b].T)
        in_maps.append(m)
    res = run_bass_kernel_spmd(nc, in_maps, core_ids=list(range(NCORES)))
    kernel.last_results = res
    out = np.stack([np.ascontiguousarray(np.asarray(r["outT"]).T) for r in res.results], axis=0)
    return out.astype(np.float32)
```

```python
import numpy as np
from contextlib import ExitStack
import concourse.bass as bass
import concourse.mybir as mybir
from concourse.bass_utils import run_bass_kernel_spmd

F32 = mybir.dt.float32
BF16 = mybir.dt.bfloat16
AF = mybir.ActivationFunctionType
ALU = mybir.AluOpType

SEQ = 2048
DM = 2048
DFF = 5632
KC = 16
NCORES = 8
EPS = 1e-6
NEG = -30000.0


class Res:
    __slots__ = ("lw", "rd")

    def __init__(self):
        self.lw = None
        self.rd = []


class Op:
    __slots__ = ("eng", "fn", "dma", "deps", "sig", "idx")

    def __init__(self, eng, fn, dma):
        self.eng = eng
        self.fn = fn
        self.dma = dma
        self.deps = []
        self.sig = None


class Sched:
    def __init__(self, nc, es, strict=True):
        self.nc = nc
        self.strict = strict
        self.engs = {"pe": nc.tensor, "act": nc.scalar, "dve": nc.vector, "pool": nc.gpsimd, "sp": nc.sync}
        nd = {"sp": 40, "pool": 24, "act": 8}
        self.dq = {}
        for q, n in nd.items():
            self.dq[q] = {"sems": [es.enter_context(nc.semaphore(f"dq_{q}_{i}")) for i in range(n)],
                          "cnt": [0] * n, "last": [None] * n, "next": 0}
        self.csem = {e: es.enter_context(nc.semaphore(f"cs_{e}")) for e in ("pe", "act", "dve", "pool")}
        self.bar = es.enter_context(nc.semaphore("phase_bar"))
        self.ccount = {e: 0 for e in self.csem}
        self.waited = {e: {} for e in self.engs}
        self.nphase = 0
        self.ops = []
        self.tot_ops = 0
        self.tot_waits = 0

    def begin_phase(self):
        self.ops = []

    def add(self, eng, fn, reads=(), writes=(), dma=False):
        op = Op(eng, fn(), dma)
        op.idx = len(self.ops)
        deps = set()
        for r in reads:
            if r.lw is not None:
                deps.add(r.lw)
        for w in writes:
            if w.lw is not None:
                deps.add(w.lw)
            deps.update(w.rd)
        for r in reads:
            if not dma:
                r.rd = [x for x in r.rd if not (self.ops[x].eng == eng and not self.ops[x].dma)]
            r.rd.append(op.idx)
        for w in writes:
            w.lw = op.idx
            w.rd = []
        deps.discard(op.idx)
        op.deps = sorted(deps)
        self.ops.append(op)
        return op

    def _wait(self, engname, sem, val):
        key = id(sem)
        if self.waited[engname].get(key, 0) >= val:
            return
        self.engs[engname].wait_ge(sem, val)
        self.waited[engname][key] = val
        self.tot_waits += 1

    def end_phase(self):
        ops = self.ops
        needed = set()
        last_on = {}
        for op in ops:
            nd = []
            for j in op.deps:
                d = ops[j]
                if (not d.dma) and (not op.dma) and d.eng == op.eng:
                    if op.eng == "pe" or not self.strict:
                        continue
                nd.append(j)
            op.deps = nd
            needed.update(nd)
            if not op.dma:
                last_on[op.eng] = op.idx
        needed.update(last_on.values())
        for op in ops:
            q = None
            deps = list(op.deps)
            if op.dma:
                q = self.dq[op.eng]
                m = q["next"]
                q["next"] = (m + 1) % len(q["sems"])
                if q["last"][m] is not None:
                    self._wait(op.eng, q["sems"][m], q["cnt"][m])
            for j in deps:
                sem, val = ops[j].sig
                self._wait(op.eng, sem, val)
            meth, a_, k_ = op.fn
            inst = meth(*a_, **k_)
            if op.dma:
                q["cnt"][m] += 16
                inst.then_inc(q["sems"][m], 16)
                op.sig = (q["sems"][m], q["cnt"][m])
                q["last"][m] = op.idx
            elif op.idx in needed:
                self.ccount[op.eng] += 1
                inst.then_inc(self.csem[op.eng], 1)
                op.sig = (self.csem[op.eng], self.ccount[op.eng])
        for qn, q in self.dq.items():
            for m, sem in enumerate(q["sems"]):
                if q["cnt"][m] > 0:
                    self._wait("sp", sem, q["cnt"][m])
        for e, sem in self.csem.items():
            if self.ccount[e] > 0:
                self._wait("sp", sem, self.ccount[e])
        self.nphase += 1
        self.engs["sp"].sem_inc(self.bar, 1)
        for e in ("pe", "act", "dve", "pool", "sp"):
            self.engs[e].wait_ge(self.bar, self.nphase)
        self.tot_ops += len(ops)
        self.ops = []


class Slots:
    def __init__(self, nc, es, name, shape, dtype, n):
        self.t = [es.enter_context(nc.sbuf_tensor(f"{name}{i}", shape, dtype)) for i in range(n)]
        self.r = [Res() for _ in range(n)]
        self.i = 0

    def next(self):
        s = self.i % len(self.t)
        self.i += 1
        return self.t[s], self.r[s]


def pipeline(items, lookahead):
    handles = {}
    n = len(items)
    for i in range(min(lookahead, n)):
        handles[i] = items[i][0]()
    for i in range(n):
        j = i + lookahead
        if j < n:
            handles[j] = items[j][0]()
        items[i][1](handles.pop(i))


class _RecEng:
    def __init__(self, real):
        self._real = real

    def __getattr__(self, name):
        real = getattr(self._real, name)

        def rec(*a, **k):
            return (real, a, k)
        return rec


class RecNC:
    def __init__(self, nc):
        self._nc = nc
        self.tensor = _RecEng(nc.tensor)
        self.scalar = _RecEng(nc.scalar)
        self.vector = _RecEng(nc.vector)
        self.gpsimd = _RecEng(nc.gpsimd)
        self.sync = _RecEng(nc.sync)

    def __getattr__(self, name):
        return getattr(self._nc, name)

    _uid = [0]

    def sbuf_tensor(self, name, shape, dtype):
        RecNC._uid[0] += 1
        return self._nc.sbuf_tensor(f"{name}_{RecNC._uid[0]}", shape, dtype)

    def psum_tensor(self, name, shape, dtype):
        RecNC._uid[0] += 1
        return self._nc.psum_tensor(f"{name}_{RecNC._uid[0]}", shape, dtype)


class KB:
    def __init__(self, nc, es):
        self.nc = RecNC(nc)
        self.S = Sched(nc, es)
        self.dres = {}
        self.x = XState()

    def dr(self, key):
        r = self.dres.get(key)
        if r is None:
            r = self.dres[key] = Res()
        return r

    def begin(self):
        self.S.begin_phase()
        self.dres = {}
        self.r_smalls, self.r_ones = Res(), Res()

    def end(self):
        self.S.end_phase()


def psum_banks(nc, es, n=8):
    return [es.enter_context(nc.psum_tensor(f"ps{i}", [128, 512], F32)) for i in range(n)], [Res() for _ in range(n)]


class XState:
    def __init__(self):
        self.es = None
        self.XN = None
        self.RS = None
        self.valid = False


def x_open(kb):
    xs = kb.x
    if xs.es is None:
        xs.es = ExitStack()
        xs.XN = xs.es.enter_context(kb.nc.sbuf_tensor("XN", [128, KC, SEQ], BF16))
        xs.RS = xs.es.enter_context(kb.nc.sbuf_tensor("RS", [128, SEQ], F32))
        xs.valid = False


def x_close(kb):
    xs = kb.x
    if xs.es is not None:
        xs.es.close()
        xs.es = None
        xs.valid = False


def sumsq_accumulate(kb, tx, rx, acc, racc, tmps, first):
    nc, S = kb.nc, kb.S
    for tt in range(4):
        sl = slice(tt * 512, (tt + 1) * 512)
        if first:
            S.add("act", lambda: nc.scalar.activation(out=acc[:, sl], in_=tx[:, sl], func=AF.Square), reads=[rx], writes=[racc])
        else:
            tmp, rtmp = tmps.next()
            S.add("act", lambda: nc.scalar.activation(out=tmp[:], in_=tx[:, sl], func=AF.Square), reads=[rx], writes=[rtmp])
            S.add("pool", lambda: nc.gpsimd.tensor_tensor(out=acc[:, sl], in0=acc[:, sl], in1=tmp[:], op=ALU.add), reads=[rtmp, racc], writes=[racc])


def rstd_finalize(kb, acc, racc, accb, raccb, ps, rps, RSTD):
    nc, S = kb.nc, kb.S
    S.add("act", lambda: nc.scalar.copy(out=accb, in_=acc[:]), reads=[racc], writes=[raccb])
    for tt in range(4):
        sl = slice(tt * 512, (tt + 1) * 512)
        S.add("pe", lambda: nc.tensor.matmul(ps[tt][:], lhsT=kb.ones_bf[:], rhs=accb[:, sl], start=True, stop=True), reads=[raccb, kb.r_ones], writes=[rps[tt]])
        S.add("act", lambda: nc.scalar.activation(out=acc[:, sl], in_=ps[tt][:], func=AF.Ln, scale=1.0 / DM, bias=EPS), reads=[rps[tt], racc], writes=[racc])
    S.add("act", lambda: nc.scalar.activation(out=acc[:], in_=acc[:], func=AF.Exp, scale=-0.5), reads=[racc], writes=[racc])
    S.add("sp", lambda: nc.sync.dma_start(out=RSTD, in_=acc[:]), reads=[racc], writes=[kb.dr("rstd")], dma=True)


def produce_next(kb, tx, rx, dc, g_next, rxn, r_rs, tmps):
    nc, S = kb.nc, kb.S
    XN, RS = kb.x.XN, kb.x.RS
    S.add("dve", lambda: nc.vector.tensor_scalar(out=XN[:, dc, :], in0=tx[:], scalar1=g_next[:, dc:dc + 1], scalar2=None, op0=ALU.mult),
          reads=[rx, kb.r_smalls], writes=[rxn[dc]])
    sumsq_accumulate(kb, tx, rx, RS, r_rs, tmps, dc == 0)


def norm_from_dram(kb, x_in, xkey, gcol, rxn, r_rs, xs, sqs, ps, rps, RSTD):
    nc, S = kb.nc, kb.S
    XN, RS = kb.x.XN, kb.x.RS
    def mk(c):
        def load():
            t, r = xs.next()
            S.add("sp", lambda: nc.sync.dma_start(out=t[:], in_=x_in[c * 128:(c + 1) * 128, :]), reads=[kb.dr((xkey, c))], writes=[r], dma=True)
            return t, r
        def comp(h):
            t, r = h
            sq, rsq = sqs.next()
            S.add("act", lambda: nc.scalar.activation(out=sq[:], in_=t[:], func=AF.Square), reads=[r], writes=[rsq])
            for tt in range(4):
                S.add("pe", lambda: nc.tensor.matmul(ps[tt][:], lhsT=kb.ones_bf[:], rhs=sq[:, tt * 512:(tt + 1) * 512], start=(c == 0), stop=(c == KC - 1)),
                      reads=[rsq, kb.r_ones], writes=[rps[tt]])
            S.add("dve", lambda: nc.vector.tensor_scalar(out=XN[:, c, :], in0=t[:], scalar1=gcol[:, c:c + 1], scalar2=None, op0=ALU.mult),
                  reads=[r, kb.r_smalls], writes=[rxn[c]])
        return load, comp
    pipeline([mk(c) for c in range(KC)], 2)
    for tt in range(4):
        sl = slice(tt * 512, (tt + 1) * 512)
        S.add("act", lambda: nc.scalar.activation(out=RS[:, sl], in_=ps[tt][:], func=AF.Ln, scale=1.0 / DM, bias=EPS), reads=[rps[tt]], writes=[r_rs])
    S.add("act", lambda: nc.scalar.activation(out=RS[:], in_=RS[:], func=AF.Exp, scale=-0.5), reads=[r_rs], writes=[r_rs])
    S.add("sp", lambda: nc.sync.dma_start(out=RSTD, in_=RS[:]), reads=[r_rs], writes=[kb.dr("rstd")], dma=True)


def ffn_phase(kb, x_in, xkey_in, x_out, xkey_out, gcol, w_in, w_out, RSTD, g_next=None):
    nc, S = kb.nc, kb.S
    NG, GS = 4, 11
    x_open(kb)
    XN, RS = kb.x.XN, kb.x.RS
    with ExitStack() as es:
        kb.begin()
        ps, rps = psum_banks(nc, es)
        rxn = [Res() for _ in range(KC)]
        r_rs = Res()
        hT = es.enter_context(nc.sbuf_tensor("hT", [128, GS, SEQ], BF16))
        rhT = [Res() for _ in range(GS)]
        xs = Slots(nc, es, "xs", [128, SEQ], F32, 3)
        sgs = Slots(nc, es, "sg", [128, 512], F32, 4)
        win = Slots(nc, es, "win", [128, KC, 128], BF16, 6)
        wout = Slots(nc, es, "wout", [128, GS, 128], BF16, 3)

        class SqS:
            i = 0
            def next(self):
                j = self.i % 2
                self.i += 1
                return hT[:, j, :], rhT[j]
        if not kb.x.valid:
            norm_from_dram(kb, x_in, xkey_in, gcol, rxn, r_rs, xs, SqS(), ps, rps, RSTD)

        w_in_v = w_in.rearrange("(k p) n -> p k n", p=128)
        w_out_v = w_out.rearrange("(f p) n -> p f n", p=128)
        items = []
        for g in range(NG):
            for j in range(GS):
                f = g * GS + j
                def load(f=f):
                    tg, rg = win.next()
                    S.add("pool", lambda: nc.gpsimd.dma_start(out=tg[:], in_=w_in_v[:, :, f * 128:(f + 1) * 128]), writes=[rg], dma=True)
                    tu, ru = win.next()
                    S.add("pool", lambda: nc.gpsimd.dma_start(out=tu[:], in_=w_in_v[:, :, DFF + f * 128:DFF + (f + 1) * 128]), writes=[ru], dma=True)
                    return tg, rg, tu, ru
                def comp(h, j=j):
                    tg, rg, tu, ru = h
                    for half in range(2):
                        b0 = half * 4
                        for k in range(KC):
                            for wi, (wt, wr) in enumerate(((tg, rg), (tu, ru))):
                                for tt in range(2):
                                    b = b0 + wi * 2 + tt
                                    tok = slice(half * 1024 + tt * 512, half * 1024 + (tt + 1) * 512)
                                    S.add("pe", lambda: nc.tensor.matmul(ps[b][:], lhsT=wt[:, k, :], rhs=XN[:, k, tok], start=(k == 0), stop=(k == KC - 1)),
                                          reads=[wr, rxn[k]], writes=[rps[b]])
                        for tt in range(2):
                            bg, bu = b0 + tt, b0 + 2 + tt
                            tok = slice(half * 1024 + tt * 512, half * 1024 + (tt + 1) * 512)
                            sg, rsg = sgs.next()
                            su, rsu = sgs.next()
                            S.add("dve", lambda: nc.vector.tensor_tensor(out=sg[:], in0=ps[bg][:], in1=RS[:, tok], op=ALU.mult), reads=[rps[bg], r_rs], writes=[rsg])
                            S.add("act", lambda: nc.scalar.activation(out=sg[:], in_=sg[:], func=AF.Silu), reads=[rsg], writes=[rsg])
                            S.add("dve", lambda: nc.vector.tensor_tensor(out=su[:], in0=ps[bu][:], in1=RS[:, tok], op=ALU.mult), reads=[rps[bu], r_rs], writes=[rsu])
                            S.add("dve", lambda: nc.vector.tensor_tensor(out=hT[:, j, tok], in0=sg[:], in1=su[:], op=ALU.mult), reads=[rsg, rsu], writes=[rhT[j]])
                items.append((load, comp))
            for dc in range(KC):
                def load(g=g, dc=dc):
                    tw, rw = wout.next()
                    S.add("pool", lambda: nc.gpsimd.dma_start(out=tw[:], in_=w_out_v[:, g * GS:(g + 1) * GS, dc * 128:(dc + 1) * 128]), writes=[rw], dma=True)
                    tx, rx = xs.next()
                    src, key = (x_in, xkey_in) if g == 0 else (x_out, xkey_out)
                    S.add("sp", lambda: nc.sync.dma_start(out=tx[:], in_=src[dc * 128:(dc + 1) * 128, :]), reads=[kb.dr((key, dc))], writes=[rx], dma=True)
                    return tw, rw, tx, rx
                def comp(h, g=g, dc=dc):
                    tw, rw, tx, rx = h
                    b0 = (dc % 2) * 4
                    for j in range(GS):
                        for tt in range(4):
                            S.add("pe", lambda: nc.tensor.matmul(ps[b0 + tt][:], lhsT=tw[:, j, :], rhs=hT[:, j, tt * 512:(tt + 1) * 512], start=(j == 0), stop=(j == GS - 1)),
                                  reads=[rw, rhT[j]], writes=[rps[b0 + tt]])
                    for tt in range(4):
                        sl = slice(tt * 512, (tt + 1) * 512)
                        S.add("dve", lambda: nc.vector.scalar_tensor_tensor(out=tx[:, sl], in0=ps[b0 + tt][:], scalar=0.5, in1=tx[:, sl], op0=ALU.mult, op1=ALU.add),
                              reads=[rps[b0 + tt], rx], writes=[rx])
                    S.add("sp", lambda: nc.sync.dma_start(out=x_out[dc * 128:(dc + 1) * 128, :], in_=tx[:]), reads=[rx], writes=[kb.dr((xkey_out, dc))], dma=True)
                    if g_next is not None and g == NG - 1:
                        produce_next(kb, tx, rx, dc, g_next, rxn, r_rs, sgs)
                items.append((load, comp))
        pipeline(items, 2)
        if g_next is not None:
            rstd_finalize(kb, RS, r_rs, hT[:, 0, :], rhT[0], ps, rps, RSTD)
        kb.end()
    kb.x.valid = g_next is not None


def proj_phase(kb, x_in, xkey, gcol, w, fm_specs, tm_specs, RSTD):
    nc, S = kb.nc, kb.S
    x_open(kb)
    XN, RS = kb.x.XN, kb.x.RS
    with ExitStack() as es:
        kb.begin()
        ps, rps = psum_banks(nc, es)
        rxn = [Res() for _ in range(KC)]
        r_rs = Res()
        wfm = Slots(nc, es, "wfm", [128, KC, 128], BF16, 4)
        wtm = Slots(nc, es, "wtm", [128, KC, 512], BF16, 2)
        o32 = Slots(nc, es, "o32", [128, SEQ], F32, 2)
        o16 = Slots(nc, es, "o16", [128, SEQ], BF16, 2)
        otm = Slots(nc, es, "otm", [128, 512], BF16, 4)
        rcol = es.enter_context(nc.sbuf_tensor("rcol", [128, 16], F32)); r_rcol = Res()
        if not kb.x.valid:
            xs = Slots(nc, es, "xs", [128, SEQ], F32, 3)
            sqs = Slots(nc, es, "sq", [128, SEQ], BF16, 2)
            norm_from_dram(kb, x_in, xkey, gcol, rxn, r_rs, xs, sqs, ps, rps, RSTD)
        S.add("sp", lambda: nc.sync.dma_start(out=rcol[:], in_=RSTD[0:1, :].rearrange("o (b i) -> i (o b)", i=128), allow_slow_non_contiguous=True),
              reads=[kb.dr("rstd")], writes=[r_rcol], dma=True)
        wv = w.rearrange("(k p) n -> p k n", p=128)
        items = []
        cnt = [0]
        for (col0, ncols, dst, dt, scale) in fm_specs:
            def load(col0=col0, ncols=ncols):
                t, r = wfm.next()
                S.add("pool", lambda: nc.gpsimd.dma_start(out=t[:, :, 0:ncols], in_=wv[:, :, col0:col0 + ncols]), writes=[r], dma=True)
                return t, r
            def comp(h, ncols=ncols, dst=dst, dt=dt, scale=scale):
                t, r = h
                b0 = (cnt[0] % 2) * 4
                cnt[0] += 1
                for k in range(KC):
                    for tt in range(4):
                        S.add("pe", lambda: nc.tensor.matmul(ps[b0 + tt][0:ncols, :], lhsT=t[:, k, 0:ncols], rhs=XN[:, k, tt * 512:(tt + 1) * 512],
                                                             start=(k == 0), stop=(k == KC - 1)), reads=[r, rxn[k]], writes=[rps[b0 + tt]])
                ot, ro = (o32 if dt == F32 else o16).next()
                for tt in range(4):
                    sl = slice(tt * 512, (tt + 1) * 512)
                    S.add("dve", lambda: nc.vector.scalar_tensor_tensor(out=ot[0:ncols, sl], in0=ps[b0 + tt][0:ncols, :], scalar=float(scale), in1=RS[0:ncols, sl],
                                                                        op0=ALU.mult, op1=ALU.mult), reads=[rps[b0 + tt], r_rs], writes=[ro])
                S.add("sp", lambda: nc.sync.dma_start(out=dst, in_=ot[0:ncols, :]), reads=[ro], dma=True)
            items.append((load, comp))
        for (col0, dst) in tm_specs:
            def load(col0=col0):
                t, r = wtm.next()
                S.add("pool", lambda: nc.gpsimd.dma_start(out=t[:], in_=wv[:, :, col0:col0 + 512]), writes=[r], dma=True)
                return t, r
            def comp(h, dst=dst):
                t, r = h
                for tb in range(16):
                    b = tb % 8
                    for k in range(KC):
                        S.add("pe", lambda: nc.tensor.matmul(ps[b][:], lhsT=XN[:, k, tb * 128:(tb + 1) * 128], rhs=t[:, k, :], start=(k == 0), stop=(k == KC - 1)),
                              reads=[r, rxn[k]], writes=[rps[b]])
                    ot, ro = otm.next()
                    if tb % 2 == 0:
                        S.add("act", lambda: nc.scalar.activation(out=ot[:], in_=ps[b][:], func=AF.Copy, scale=rcol[:, tb:tb + 1]), reads=[rps[b], r_rcol], writes=[ro])
                    else:
                        S.add("dve", lambda: nc.vector.tensor_scalar(out=ot[:], in0=ps[b][:], scalar1=rcol[:, tb:tb + 1], scalar2=None, op0=ALU.mult),
                              reads=[rps[b], r_rcol], writes=[ro])
                    S.add("sp", lambda: nc.sync.dma_start(out=dst[tb * 128:(tb + 1) * 128, :], in_=ot[:]), reads=[ro], dma=True)
            items.append((load, comp))
        pipeline(items, 1)
        kb.end()
    x_close(kb)


def outproj_phase(kb, x_in, xkey_in, x_out, xkey_out, Y, w, RSTD, g_next=None):
    nc, S = kb.nc, kb.S
    if g_next is not None:
        x_open(kb)
    with ExitStack() as es:
        kb.begin()
        ps, rps = psum_banks(nc, es)
        rxn = [Res() for _ in range(KC)]
        r_rs = Res()
        yb = es.enter_context(nc.sbuf_tensor("yb", [128, KC, SEQ], BF16))
        ryb = [Res() for _ in range(KC)]
        xs = Slots(nc, es, "xs", [128, SEQ], F32, 3)
        ws = Slots(nc, es, "ws", [128, KC, 128], BF16, 3)
        accb = es.enter_context(nc.sbuf_tensor("accb", [128, SEQ], BF16)); raccb = Res()
        sqt = Slots(nc, es, "sqt", [128, 512], F32, 2)
        for c in range(KC):
            S.add("sp", lambda: nc.sync.dma_start(out=yb[:, c, :], in_=Y[c * 128:(c + 1) * 128, :]), writes=[ryb[c]], dma=True)
        wv = w.rearrange("(k p) n -> p k n", p=128)
        items = []
        for dc in range(KC):
            def load(dc=dc):
                tw, rw = ws.next()
                S.add("pool", lambda: nc.gpsimd.dma_start(out=tw[:], in_=wv[:, :, dc * 128:(dc + 1) * 128]), writes=[rw], dma=True)
                tx, rx = xs.next()
                S.add("sp", lambda: nc.sync.dma_start(out=tx[:], in_=x_in[dc * 128:(dc + 1) * 128, :]), writes=[rx], dma=True)
                return tw, rw, tx, rx
            def comp(h, dc=dc):
                tw, rw, tx, rx = h
                b0 = (dc % 2) * 4
                for k in range(KC):
                    for tt in range(4):
                        S.add("pe", lambda: nc.tensor.matmul(ps[b0 + tt][:], lhsT=tw[:, k, :], rhs=yb[:, k, tt * 512:(tt + 1) * 512], start=(k == 0), stop=(k == KC - 1)),
                              reads=[rw, ryb[k]], writes=[rps[b0 + tt]])
                for tt in range(4):
                    sl = slice(tt * 512, (tt + 1) * 512)
                    S.add("dve", lambda: nc.vector.tensor_tensor(out=tx[:, sl], in0=ps[b0 + tt][:], in1=tx[:, sl], op=ALU.add), reads=[rps[b0 + tt], rx], writes=[rx])
                S.add("sp", lambda: nc.sync.dma_start(out=x_out[dc * 128:(dc + 1) * 128, :], in_=tx[:]), reads=[rx], dma=True)
                if g_next is not None:
                    produce_next(kb, tx, rx, dc, g_next, rxn, r_rs, sqt)
            items.append((load, comp))
        pipeline(items, 2)
        if g_next is not None:
            rstd_finalize(kb, kb.x.RS, r_rs, accb[:], raccb, ps, rps, RSTD)
        kb.end()
    kb.x.valid = g_next is not None


def abrg_phase(kb, PROJ, Y, d):
    nc, S = kb.nc, kb.S
    sm = kb.smalls
    T = SEQ
    def smc(name, j):
        o = SM[name][0] + j
        return sm[:, o:o + 1]
    with ExitStack() as es:
        kb.begin()
        ps, rps = psum_banks(nc, es)
        rs = kb.r_smalls
        wr = es.enter_context(nc.sbuf_tensor("wr", [128, 8, 128], BF16))
        wi = es.enter_context(nc.sbuf_tensor("wi", [128, 8, 128], BF16))
        r_wr, r_wi = Res(), Res()
        S.add("pool", lambda: nc.gpsimd.dma_start(out=wr[:], in_=d["rg_w_r"][0].rearrange("g c e -> c g e")), writes=[r_wr], dma=True)
        S.add("pool", lambda: nc.gpsimd.dma_start(out=wi[:], in_=d["rg_w_i"][0].rearrange("g c e -> c g e")), writes=[r_wi], dma=True)
        cl = es.enter_context(nc.sbuf_tensor("cl", [128, 4, 8], F32))
        r_cl = Res()
        lo = SM["lam"][0]
        lam = sm[:, lo:lo + 8]
        S.add("dve", lambda: nc.vector.tensor_scalar(out=cl[:, 0, :], in0=lam, scalar1=-1.0, scalar2=None, op0=ALU.mult), reads=[rs], writes=[r_cl])
        S.add("dve", lambda: nc.vector.tensor_tensor(out=cl[:, 0, :], in0=cl[:, 0, :], in1=lam, op=ALU.max), reads=[rs, r_cl], writes=[r_cl])
        S.add("act", lambda: nc.scalar.activation(out=cl[:, 0, :], in_=cl[:, 0, :], func=AF.Exp, scale=-1.0), reads=[r_cl], writes=[r_cl])
        S.add("act", lambda: nc.scalar.activation(out=cl[:, 0, :], in_=cl[:, 0, :], func=AF.Ln, bias=1.0, scale=1.0), reads=[r_cl], writes=[r_cl])
        S.add("dve", lambda: nc.vector.tensor_scalar(out=cl[:, 1, :], in0=lam, scalar1=-1.0, scalar2=0.0, op0=ALU.mult, op1=ALU.max), reads=[rs, r_cl], writes=[r_cl])
        S.add("dve", lambda: nc.vector.tensor_tensor(out=cl[:, 1, :], in0=cl[:, 1, :], in1=cl[:, 0, :], op=ALU.add), reads=[r_cl], writes=[r_cl])
        S.add("dve", lambda: nc.vector.tensor_scalar(out=cl[:, 2, :], in0=cl[:, 1, :], scalar1=-8.0, scalar2=None, op0=ALU.mult), reads=[r_cl], writes=[r_cl])
        S.add("dve", lambda: nc.vector.tensor_scalar(out=cl[:, 3, :], in0=cl[:, 1, :], scalar1=-16.0, scalar2=None, op0=ALU.mult), reads=[r_cl], writes=[r_cl])
        NS = 3
        gts = Slots(nc, es, "gt", [128, T], F32, NS)
        xps = Slots(nc, es, "xp", [128, T + 4], F32, NS)
        xcs = Slots(nc, es, "xc", [128, T], F32, NS)
        xbs = Slots(nc, es, "xb", [128, T], BF16, 2)
        rrs = Slots(nc, es, "rr", [128, T], F32, NS)
        iis = Slots(nc, es, "ii", [128, T], F32, NS)
        b1s = Slots(nc, es, "b1", [128, T], F32, NS)
        ybs = Slots(nc, es, "yo", [128, T], BF16, 2)
        for i in range(NS):
            S.add("dve", lambda i=i: nc.vector.memset(xps.t[i][:, 0:4], 0.0), writes=[xps.r[i]])

        def stA(u):
            c = u["c"]
            gt, rg = gts.next()
            S.add("sp", lambda: nc.sync.dma_start(out=gt[:], in_=PROJ[c * 128:(c + 1) * 128, :]), writes=[rg], dma=True)
            xp, rx = xps.next()
            S.add("sp", lambda: nc.sync.dma_start(out=xp[:, 4:4 + T], in_=PROJ[1024 + c * 128:1024 + (c + 1) * 128, :]), reads=[rx], writes=[rx], dma=True)
            xc, rxc = xcs.next(); xb, rxb = xbs.next(); rr, rrr = rrs.next(); ii, rii = iis.next()
            u.update(gt=gt, rg=rg, xc=xc, rxc=rxc, rr=rr, rrr=rrr, ii=ii, rii=rii)
            S.add("dve", lambda: nc.vector.tensor_scalar(out=xc[:], in0=xp[:, 4:4 + T], scalar1=smc("cw", c * 4 + 3), scalar2=smc("cb", c),
                                                         op0=ALU.mult, op1=ALU.add), reads=[rx, rs], writes=[rxc])
            for j in range(3):
                S.add("dve", lambda: nc.vector.scalar_tensor_tensor(out=xc[:], in0=xp[:, 1 + j:1 + j + T], scalar=smc("cw", c * 4 + j), in1=xc[:],
                                                                    op0=ALU.mult, op1=ALU.add), reads=[rx, rs, rxc], writes=[rxc])
            S.add("act", lambda: nc.scalar.copy(out=xb[:], in_=xc[:]), reads=[rxc], writes=[rxb])
            for tt in range(4):
                sl = slice(tt * 512, (tt + 1) * 512)
                S.add("pe", lambda: nc.tensor.matmul(ps[tt][:], lhsT=wr[:, c, :], rhs=xb[:, sl], start=True, stop=True), reads=[r_wr, rxb], writes=[rps[tt]])
                S.add("pe", lambda: nc.tensor.matmul(ps[4 + tt][:], lhsT=wi[:, c, :], rhs=xb[:, sl], start=True, stop=True), reads=[r_wi, rxb], writes=[rps[4 + tt]])
            for tt in range(4):
                sl = slice(tt * 512, (tt + 1) * 512)
                S.add("act", lambda: nc.scalar.activation(out=rr[:, sl], in_=ps[tt][:], func=AF.Sigmoid, bias=smc("br", c), scale=1.0), reads=[rps[tt], rs], writes=[rrr])
                S.add("act", lambda: nc.scalar.activation(out=ii[:, sl], in_=ps[4 + tt][:], func=AF.Sigmoid, bias=smc("bi", c), scale=1.0), reads=[rps[4 + tt], rs], writes=[rii])

        def stB(u):
            c, xc, rxc, rr, rrr, ii, rii = u["c"], u["xc"], u["rxc"], u["rr"], u["rrr"], u["ii"], u["rii"]
            b1, rb1 = b1s.next()
            u.update(b1=b1, rb1=rb1)
            S.add("act", lambda: nc.scalar.activation(out=b1[:], in_=rr[:], func=AF.Exp, scale=cl[:, 3, c:c + 1]), reads=[rrr, r_cl], writes=[rb1])
            S.add("act", lambda: nc.scalar.activation(out=rr[:], in_=rr[:], func=AF.Exp, scale=cl[:, 2, c:c + 1]), reads=[rrr, r_cl], writes=[rrr])
            S.add("pool", lambda: nc.gpsimd.tensor_scalar(out=b1[:], in0=b1[:], scalar1=-1.0, scalar2=1.0, op0=ALU.mult, op1=ALU.add), reads=[rb1], writes=[rb1])
            S.add("act", lambda: nc.scalar.activation(out=b1[:], in_=b1[:], func=AF.Sqrt), reads=[rb1], writes=[rb1])
            S.add("pool", lambda: nc.gpsimd.tensor_tensor(out=ii[:], in0=ii[:], in1=b1[:], op=ALU.mult), reads=[rii, rb1], writes=[rii])
            S.add("pool", lambda: nc.gpsimd.tensor_tensor(out=ii[:], in0=ii[:], in1=xc[:], op=ALU.mult), reads=[rii, rxc], writes=[rii])
            S.add("dve", lambda: nc.vector.tensor_tensor_scan(out=b1[:], data0=rr[:], data1=ii[:], initial=0.0, op0=ALU.mult, op1=ALU.add),
                  reads=[rrr, rii, rb1], writes=[rb1])

        def stC(u):
            c, gt, rg, xc, rxc, b1, rb1 = u["c"], u["gt"], u["rg"], u["xc"], u["rxc"], u["b1"], u["rb1"]
            yo, ryo = ybs.next()
            S.add("act", lambda: nc.scalar.activation(out=xc[:], in_=gt[:], func=AF.Gelu_apprx_tanh), reads=[rg, rxc], writes=[rxc])
            S.add("dve", lambda: nc.vector.tensor_tensor(out=yo[:], in0=xc[:], in1=b1[:], op=ALU.mult), reads=[rxc, rb1], writes=[ryo])
            S.add("sp", lambda: nc.sync.dma_start(out=Y[c * 128:(c + 1) * 128, :], in_=yo[:]), reads=[ryo], dma=True)

        skewed([dict(c=c) for c in range(8)], [stA, stB, stC])
        kb.end()


def abattn_phase(kb, PROJ, VT, Y, ZT, d):
    nc, S = kb.nc, kb.S
    sm = kb.smalls
    T = SEQ
    with ExitStack() as es:
        kb.begin()
        rs_ = kb.r_smalls
        ps, rps = psum_banks(nc, es)
        rb = es.enter_context(nc.sbuf_tensor("rb", [33, 8], F32)); r_rb = Res()
        selT = es.enter_context(nc.sbuf_tensor("selT", [33, 1152], F32)); r_sel = Res()
        ones33 = es.enter_context(nc.sbuf_tensor("ones33", [33, 128], F32)); r_o33 = Res()
        gqs = es.enter_context(nc.sbuf_tensor("gqs", [128, 2], F32)); r_gqs = Res()
        lhs = Slots(nc, es, "lh", [33, 128], F32, 2)
        zs = Slots(nc, es, "zs", [128, 1152], F32, 2)
        S.add("dve", lambda: nc.vector.memset(rb[:], 1.0), writes=[r_rb])
        S.add("sp", lambda: nc.sync.dma_start(out=rb[0:32, :], in_=d["rel_bias"]), reads=[r_rb], writes=[r_rb], dma=True)
        S.add("sp", lambda: nc.sync.dma_start(out=selT[:], in_=d["selT"]), writes=[r_sel], dma=True)
        S.add("dve", lambda: nc.vector.memset(ones33[:], 1.0), writes=[r_o33])
        go = SM["gqk"][0]
        S.add("dve", lambda: nc.vector.tensor_scalar(out=gqs[:, 0:1], in0=sm[:, go:go + 1], scalar1=float(128.0 ** -0.5), scalar2=None, op0=ALU.mult),
              reads=[rs_], writes=[r_gqs])
        S.add("dve", lambda: nc.vector.tensor_copy(out=gqs[:, 1:2], in_=sm[:, go + 1:go + 2]), reads=[rs_, r_gqs], writes=[r_gqs])
        for h in range(8):
            lh, rlh = lhs.next()
            S.add("dve", lambda h=h, lh=lh: nc.vector.tensor_scalar(out=lh[:], in0=ones33[:], scalar1=rb[:, h:h + 1], scalar2=None, op0=ALU.mult),
                  reads=[r_o33, r_rb], writes=[rlh])
            z, rz = zs.next()
            for p in range(3):
                S.add("pe", lambda p=p, lh=lh: nc.tensor.matmul(ps[p][:, 0:384], lhsT=lh[:], rhs=selT[:, p * 384:(p + 1) * 384], start=True, stop=True),
                      reads=[rlh, r_sel], writes=[rps[p]])
                S.add("act", lambda p=p, z=z: nc.scalar.copy(out=z[:, p * 384:(p + 1) * 384], in_=ps[p][:, 0:384]), reads=[rps[p]], writes=[rz])
            S.add("sp", lambda h=h, z=z: nc.sync.dma_start(out=ZT[h], in_=z[:]), reads=[rz], writes=[kb.dr(("zt", h))], dma=True)

        qfs = Slots(nc, es, "qf", [128, T], F32, 2)
        sqs = Slots(nc, es, "sq", [128, T], BF16, 2)
        rss = Slots(nc, es, "rs", [128, T], F32, 2)
        qns = Slots(nc, es, "qn", [128, T], BF16, 4)
        vts = Slots(nc, es, "vt", [128, 16, 128], BF16, 6)
        b4s = Slots(nc, es, "b4", [128, 512], F32, 12)
        tmps = Slots(nc, es, "tmp", [128, 512], F32, 4)
        pt3s = Slots(nc, es, "pt3", [128, T], BF16, 2)
        ptgs = Slots(nc, es, "ptg", [128, T], BF16, 3)
        rds = Slots(nc, es, "rd", [128, 512], F32, 2)
        yhs = Slots(nc, es, "yh", [128, T], BF16, 2)
        qkb = [0]
        HD = {}
        QR = {}
        for sl_ in (pt3s, ptgs, yhs):
            for t_ in sl_.t:
                QR[id(t_)] = [Res() for _ in range(4)]

        def prep(u):
            h = u["h"]
            qk = []
            for w_ in range(2):
                t, r = qfs.next()
                S.add("sp", lambda: nc.sync.dma_start(out=t[:], in_=PROJ[2048 + w_ * 1024 + h * 128:2048 + w_ * 1024 + (h + 1) * 128, :]), writes=[r], dma=True)
                qk.append((t, r))
            hs = slice(h * 128, (h + 1) * 128)
            vt1, rv1 = vts.next(); vt2, rv2 = vts.next(); vt3, rv3 = vts.next()
            S.add("sp", lambda: nc.sync.dma_start(out=vt1[:], in_=VT.rearrange("(b i) f -> i b f", i=128)[:, :, hs]), writes=[rv1], dma=True)
            for n in range(4):
                S.add("sp", lambda: nc.sync.dma_start(out=vt2[:, n * 4:(n + 1) * 4, :], in_=VT[512 * n:512 * (n + 1), hs].rearrange("(i r) f -> i r f", r=4)), writes=[rv2], dma=True)
            S.add("sp", lambda: nc.sync.dma_start(out=vt3[:], in_=VT.rearrange("(i r) f -> i r f", r=16)[:, :, hs]), writes=[rv3], dma=True)
            B4 = {}
            for p in range(3):
                for part, base in (("cur", 127), ("prev", 255)):
                    t, r = b4s.next()
                    src = bass.AP(ZT.tensor, h * 128 * 1152 + p * 384 + base, [[1151, 128], [1, 128]])
                    S.add("sp", lambda: nc.sync.dma_start(out=t[:, 0:128], in_=src), reads=[kb.dr(("zt", h))], writes=[r], dma=True)
                    B4[(p, part)] = (t, r)
            nrm = []
            for w_ in range(2):
                t, r = qk[w_]
                sq, rsq = sqs.next()
                S.add("act", lambda: nc.scalar.activation(out=sq[:], in_=t[:], func=AF.Square), reads=[r], writes=[rsq])
                for tt in range(4):
                    S.add("pe", lambda: nc.tensor.matmul(ps[tt][:], lhsT=kb.ones_bf[:], rhs=sq[:, tt * 512:(tt + 1) * 512], start=True, stop=True),
                          reads=[rsq, kb.r_ones], writes=[rps[tt]])
                rs, rrs = rss.next()
                for tt in range(4):
                    S.add("act", lambda: nc.scalar.activation(out=rs[:, tt * 512:(tt + 1) * 512], in_=ps[tt][:], func=AF.Ln, scale=1.0 / 128, bias=EPS),
                          reads=[rps[tt]], writes=[rrs])
                S.add("act", lambda: nc.scalar.activation(out=rs[:], in_=rs[:], func=AF.Exp, scale=-0.5), reads=[rrs], writes=[rrs])
                qn, rqn = qns.next()
                S.add("dve", lambda: nc.vector.scalar_tensor_tensor(out=qn[:], in0=t[:], scalar=gqs[:, w_:w_ + 1], in1=rs[:], op0=ALU.mult, op1=ALU.mult),
                      reads=[r, rrs, r_gqs], writes=[rqn])
                nrm.append((qn, rqn))
            HD[h] = dict(qn=nrm[0], kn=nrm[1], vt=((vt1, rv1), (vt2, rv2), (vt3, rv3)), B4=B4)

        def qk_batch(h, pairs, bkey, dst, rdst):
            qn, rqn = HD[h]["qn"]; kn, rkn = HD[h]["kn"]
            b = qkb[0] % 4
            qkb[0] += 1
            n = len(pairs)
            for i, (ksl, qsl) in enumerate(pairs):
                S.add("pe", lambda: nc.tensor.matmul(ps[b][:, i * 128:(i + 1) * 128], lhsT=kn[:, ksl], rhs=qn[:, qsl], start=True, stop=True),
                      reads=[rkn, rqn], writes=[rps[b]])
            tmp, rtmp = tmps.next()
            bt, rbt = HD[h]["B4"][bkey]
            b1 = bt[:, 0:128]
            bb_ = bass.AP(b1.tensor, b1.offset, [list(b1.ap[0]), [0, n], [1, 128]])
            S.add("dve", lambda: nc.vector.tensor_tensor(out=tmp[:, 0:n * 128].rearrange("p (a c) -> p a c", c=128),
                                                         in0=ps[b][:, 0:n * 128].rearrange("p (a c) -> p a c", c=128), in1=bb_, op=ALU.add),
                  reads=[rps[b], rbt], writes=[rtmp])
            S.add("act", lambda: nc.scalar.activation(out=dst, in_=tmp[:, 0:n * 128], func=AF.Exp), reads=[rtmp], writes=[rdst])

        def p3A(u):
            h = u["h"]
            pt3, _r = pt3s.next()
            rpt3 = QR[id(pt3)]
            HD[h]["pt3"] = (pt3, rpt3)
            yh, _r = yhs.next()
            HD[h]["yh"] = (yh, QR[id(yh)])
            for bi in range(4):
                pairs = [(slice(r, T, 16), slice(r, T, 16)) for r in range(bi * 4, bi * 4 + 4)]
                qk_batch(h, pairs, (2, "cur"), pt3[:, bi * 512:(bi + 1) * 512], rpt3[bi])

        def gA(u):
            h, j = u["h"], u["j"]
            (vt1, rv1), (vt2, rv2), (vt3, rv3) = HD[h]["vt"]
            pt3, rpt3 = HD[h]["pt3"]
            ptg, _r = ptgs.next()
            rq4 = QR[id(ptg)]
            blk = lambda n: slice(n * 128, (n + 1) * 128)
            sub = lambda n, r: slice(512 * n + r, 512 * (n + 1), 4)
            mm = []
            qk_batch(h, [(blk(4 * j + i), blk(4 * j + i)) for i in range(4)], (0, "cur"), ptg[:, 0:512], rq4[0])
            for i in range(4):
                mm.append((slice(i * 128, (i + 1) * 128), vt1[:, 4 * j + i, :], rv1, ptg[:, i * 128:(i + 1) * 128], rq4[0]))
            pv = [i for i in range(4) if 4 * j + i >= 1]
            qk_batch(h, [(blk(4 * j + i - 1), blk(4 * j + i)) for i in pv], (0, "prev"), ptg[:, 512:512 + len(pv) * 128], rq4[1])
            for ii, i in enumerate(pv):
                mm.append((slice(i * 128, (i + 1) * 128), vt1[:, 4 * j + i - 1, :], rv1, ptg[:, 512 + ii * 128:512 + (ii + 1) * 128], rq4[1]))
            qk_batch(h, [(sub(j, r), sub(j, r)) for r in range(4)], (1, "cur"), ptg[:, 1024:1536], rq4[2])
            for r in range(4):
                mm.append((slice(r, 512, 4), vt2[:, j * 4 + r, :], rv2, ptg[:, 1024 + r * 128:1024 + (r + 1) * 128], rq4[2]))
            if j >= 1:
                qk_batch(h, [(sub(j - 1, r), sub(j, r)) for r in range(4)], (1, "prev"), ptg[:, 1536:2048], rq4[3])
                for r in range(4):
                    mm.append((slice(r, 512, 4), vt2[:, (j - 1) * 4 + r, :], rv2, ptg[:, 1536 + r * 128:1536 + (r + 1) * 128], rq4[3]))
            for r in range(16):
                mm.append((slice(r, 512, 16), vt3[:, r, :], rv3, pt3[:, r * 128 + 32 * j:r * 128 + 32 * j + 32], rpt3[r // 4]))
            u["mm"] = mm

        gcnt = [0]

        def gB(u):
            mm = u["mm"]
            bn, bd = (4, 5) if gcnt[0] % 2 == 0 else (6, 7)
            gcnt[0] += 1
            u["banks"] = (bn, bd)
            for idx, (cols_, lv, rlv, rhs, rrhs) in enumerate(mm):
                st, sp_ = idx == 0, idx == len(mm) - 1
                S.add("pe", lambda: nc.tensor.matmul(ps[bn][:, cols_], lhsT=lv, rhs=rhs, start=st, stop=sp_, skip_group_check=True), reads=[rlv, rrhs], writes=[rps[bn]])
                S.add("pe", lambda: nc.tensor.matmul(ps[bd][:, cols_], lhsT=kb.ones_bf[:], rhs=rhs, start=st, stop=sp_, skip_group_check=True),
                      reads=[kb.r_ones, rrhs], writes=[rps[bd]])

        def gC(u):
            h, j = u["h"], u["j"]
            bn, bd = u["banks"]
            yh, ryh = HD[h]["yh"]
            rd, rrd = rds.next()
            S.add("act", lambda: nc.scalar.activation(out=rd[:], in_=ps[bd][:], func=AF.Ln), reads=[rps[bd]], writes=[rrd])
            S.add("act", lambda: nc.scalar.activation(out=rd[:], in_=rd[:], func=AF.Exp, scale=-1.0), reads=[rrd], writes=[rrd])
            S.add("dve", lambda: nc.vector.tensor_tensor(out=yh[:, j * 512:(j + 1) * 512], in0=ps[bn][:], in1=rd[:], op=ALU.mult), reads=[rps[bn], rrd], writes=[ryh[j]])
            if j == 3:
                S.add("sp", lambda: nc.sync.dma_start(out=Y[(8 + h) * 128:(9 + h) * 128, :], in_=yh[:]), reads=ryh, dma=True)

        nop = lambda u: None
        units = []
        units.append(dict(h=0, st=(prep, nop, nop)))
        for h in range(8):
            units.append(dict(h=h, st=(p3A, nop, nop)))
            if h + 1 < 8:
                units.append(dict(h=h + 1, st=(prep, nop, nop)))
            for j in range(4):
                units.append(dict(h=h, j=j, st=(gA, gB, gC)))
        skewed(units, [lambda u: u["st"][0](u), lambda u: u["st"][1](u), lambda u: u["st"][2](u)])
        kb.end()


def t5_sel_table():
    pats = ((128, 1), (512, 4), (2048, 16))
    out = np.zeros((33, 3 * 384), np.float32)
    for p, (win, dil) in enumerate(pats):
        blk = win // dil
        for idx in range(384):
            rel = idx - 127
            if 0 <= rel <= blk and idx < 383:
                dist = np.int32(rel * dil)
                d_f = np.float32(max(int(dist), 1))
                large = 16 + np.int32(np.float32(np.log(np.float32(d_f / np.float32(16.0)))) / np.float32(np.log(2048.0 / 16.0)) * np.float32(16.0))
                large = min(int(large), 31)
                bkt = int(dist) if dist < 16 else large
                out[bkt, p * 384 + idx] = 1.0
            else:
                out[32, p * 384 + idx] = NEG
    return out


def cols(v, p=128):
    v = np.asarray(v, np.float32)
    n = v.shape[-1] // p
    a = v.reshape(-1, n, p)
    return np.ascontiguousarray(a.transpose(2, 0, 1).reshape(p, -1))


SM = {}


def build_smalls(inp):
    parts = []
    off = [0]

    def put(name, arr):
        arr = np.asarray(arr, np.float32)
        if arr.shape[0] < 128:
            arr = np.concatenate([arr, np.zeros((128 - arr.shape[0], arr.shape[1]), np.float32)], axis=0)
        SM[name] = (off[0], arr.shape[1])
        off[0] += arr.shape[1]
        parts.append(arr)
    put("ng", cols(inp["norm_g"]))
    cw = np.asarray(inp["ab_conv_w"], np.float32)[0]
    put("cw", np.ascontiguousarray(cw.reshape(4, 8, 128).transpose(2, 1, 0).reshape(128, 32)))
    put("cb", cols(inp["ab_conv_b"][0]))
    put("br", cols(inp["rg_b_r"][0]))
    put("bi", cols(inp["rg_b_i"][0]))
    put("lam", cols(inp["rg_lambda"][0]))
    put("gqk", np.ascontiguousarray(np.asarray(inp["qk_gain"], np.float32)[0].T))
    cw2 = np.asarray(inp["cd_conv_w"], np.float32)[0]
    put("cw2", np.ascontiguousarray(cw2.reshape(4, 16, 128).transpose(2, 1, 0).reshape(128, 64)))
    put("cb2", cols(inp["cd_conv_b"][0]))
    put("gb", np.ascontiguousarray(np.asarray(inp["mlstm_gate_bias"], np.float32)[0].T))
    put("hg", cols(inp["mlstm_h_gain"][0]))
    return np.ascontiguousarray(np.concatenate(parts, axis=1))


PHASE_GROUPS = {"ab": ("abproj", "abrg", "abattn", "about"), "cd": ("cdproj", "cdml", "cdsb", "cdout")}
ALL_PHASES = ("ffn00", "ab", "ffn01", "ffn10", "cd", "ffn11")
DEBUG_OUT = None


def build_program(nsm, phases):
    nc = bass.Bass("TRN2", target_bir_lowering=False)
    need_ffn = any(p.startswith("ffn") for p in phases)
    need_ab = "ab" in phases
    need_cd = "cd" in phases
    d = {}

    def din(name, shape):
        d[name] = nc.dram_tensor(name, list(shape), F32, kind="ExternalInput").ap()
    din("xT", [DM, SEQ])
    din("smalls", [128, nsm])
    if need_ffn:
        din("ffn_w_in", [2, 2, DM, 2 * DFF])
        din("ffn_w_out", [2, 2, DFF, DM])
    if need_ab:
        din("ab_w_in", [1, DM, 5120]); din("rg_w_r", [1, 8, 128, 128]); din("rg_w_i", [1, 8, 128, 128])
        din("rel_bias", [32, 8]); din("selT", [33, 1152]); din("ab_w_out", [1, DM, DM])
    if need_cd:
        din("cd_w_in", [1, DM, 7176]); din("cd_w_out", [1, DM, DM]); din("cdc", [128, CDC_N])
    outT = nc.dram_tensor("outT", [DM, SEQ], F32, kind="ExternalOutput").ap()

    def scratch(name, shape, dt):
        kind = "ExternalOutput" if DEBUG_OUT == name else "Internal"
        return nc.dram_tensor(name, list(shape), dt, kind=kind).ap()
    xa = scratch("xa", [DM, SEQ], F32)
    xb = scratch("xb", [DM, SEQ], F32)
    PROJ = scratch("PROJ", [4096, SEQ], F32)
    VT = scratch("VT", [SEQ, 2048], BF16)
    Y = scratch("Y", [DM, SEQ], BF16)
    ZT = scratch("ZT", [8, 128, 1152], F32)
    QKD = scratch("QKD", [2048, SEQ], BF16)
    GIF = scratch("GIF", [8, SEQ], F32)
    BROW = scratch("BROW", [4, SEQ], F32)
    C1COL = scratch("C1COL", [128, 64], F32)
    RSTD = scratch("RSTD", [128, SEQ], F32)
    with ExitStack() as es:
        kb = KB(nc, es)
        S = kb.S
        rnc = kb.nc
        smalls = es.enter_context(nc.sbuf_tensor("smalls_sb", [128, nsm], F32))
        ones_bf = es.enter_context(nc.sbuf_tensor("ones_bf", [128, 128], BF16))
        kb.smalls, kb.ones_bf = smalls, ones_bf
        kb.r_smalls, kb.r_ones = Res(), Res()
        kb.begin()
        S.add("sp", lambda: rnc.sync.dma_start(out=smalls[:], in_=d["smalls"]), writes=[kb.r_smalls], dma=True)
        S.add("dve", lambda: rnc.vector.memset(ones_bf[:], 1.0), writes=[kb.r_ones])
        kb.end()
        kb.r_smalls, kb.r_ones = Res(), Res()

        def ng(l, i):
            o = SM["ng"][0] + (l * 3 + i) * 16
            return smalls[:, o:o + 16]

        cur, curkey = d["xT"], "x0"
        bufs = [xa, xb]
        def next_gain(pi):
            if pi + 1 >= len(phases):
                return None
            nx = phases[pi + 1]
            if nx.startswith("ffn"):
                return ng(int(nx[3]), 0 if int(nx[4]) == 0 else 2)
            return ng(0, 1) if nx == "ab" else ng(1, 1)

        for pi, ph in enumerate(phases):
            last = pi == len(phases) - 1
            dst = outT if last else bufs[pi % 2]
            dkey = f"x{pi + 1}"
            gn = next_gain(pi)
            if ph.startswith("ffn"):
                l, i = int(ph[3]), int(ph[4])
                ffn_phase(kb, cur, curkey, dst, dkey, ng(l, 0 if i == 0 else 2), d["ffn_w_in"][l, i], d["ffn_w_out"][l, i], RSTD, g_next=gn)
            elif ph == "ab":
                fm = [(c * 128, 128, PROJ[c * 128:(c + 1) * 128, :], F32, 1.0) for c in range(32)]
                tm = [(4096 + g * 512, VT[:, g * 512:(g + 1) * 512]) for g in range(2)]
                proj_phase(kb, cur, curkey, ng(0, 1), d["ab_w_in"][0], fm, tm, RSTD)
                abrg_phase(kb, PROJ, Y, d)
                abattn_phase(kb, PROJ, VT, Y, ZT, d)
                outproj_phase(kb, cur, curkey, dst, dkey, Y, d["ab_w_out"][0], RSTD, g_next=gn)
            elif ph == "cd":
                cd_phases(kb, cur, curkey, dst, dkey, ng(1, 1), d, PROJ, VT, Y, QKD, GIF, BROW, C1COL, RSTD, gn)
            else:
                raise ValueError(ph)
            cur, curkey = dst, dkey
        x_close(kb)
        print("ops", S.tot_ops, "waits", S.tot_waits, "sig", S.ccount)
    return nc


CDC_N = 1412
CD_SUB = ("gate", "ml", "sb")
C_ID4, C_SEL, C_M4T, C_TRIL, C_NEGSB, C_IDENT = 0, 4, 516, 1028, 1156, 1284


def cd_consts():
    c = np.zeros((128, CDC_N), np.float32)
    c[0:4, 0:4] = np.eye(4, dtype=np.float32)
    for hh in range(4):
        c[hh, C_SEL + hh * 128:C_SEL + (hh + 1) * 128] = 1.0
    s_ = np.arange(128)[:, None]
    l_ = np.arange(128)[None, :]
    m = np.where(s_ <= l_, 0.0, NEG).astype(np.float32)
    for rep in range(4):
        c[:, C_M4T + rep * 128:C_M4T + (rep + 1) * 128] = m
    c[:, C_TRIL:C_TRIL + 128] = (l_ < s_).astype(np.float32)
    c[:, C_NEGSB:C_NEGSB + 128] = np.where(l_ < s_, 0.0, NEG)
    c[:, C_IDENT:C_IDENT + 128] = np.eye(128, dtype=np.float32)
    return c


def cdgate_phase(kb, GIF, BROW, C1COL, d):
    nc, S = kb.nc, kb.S
    T = SEQ
    sm = kb.smalls
    with ExitStack() as es:
        kb.begin()
        rs_ = kb.r_smalls
        ps0 = es.enter_context(nc.psum_tensor("psg", [128, 512], F32)); rps0 = Res()
        fb = es.enter_context(nc.sbuf_tensor("fb", [4, T], F32)); rfb = Res()
        ib = es.enter_context(nc.sbuf_tensor("ib", [4, T], F32)); rib = Res()
        rm = es.enter_context(nc.sbuf_tensor("rm", [4, T], F32)); rrm = Res()
        bb = es.enter_context(nc.sbuf_tensor("bb", [4, T], F32)); rbb = Res()
        id4 = es.enter_context(nc.sbuf_tensor("id4", [4, 4], F32)); rid = Res()
        c1 = es.enter_context(nc.sbuf_tensor("c1", [128, 64], F32)); rc1 = Res()
        go = SM["gb"][0]
        S.add("sp", lambda: nc.sync.dma_start(out=fb[:], in_=GIF[4:8, :]), writes=[rfb], dma=True)
        S.add("sp", lambda: nc.sync.dma_start(out=ib[:], in_=GIF[0:4, :]), writes=[rib], dma=True)
        S.add("sp", lambda: nc.sync.dma_start(out=id4[:], in_=d["cdc"][0:4, C_ID4:C_ID4 + 4]), writes=[rid], dma=True)
        S.add("dve", lambda: nc.vector.tensor_scalar(out=fb[:], in0=fb[:], scalar1=sm[0:4, go + 1:go + 2], scalar2=None, op0=ALU.add), reads=[rfb, rs_], writes=[rfb])
        S.add("act", lambda: nc.scalar.activation(out=fb[:], in_=fb[:], func=AF.Exp, scale=-1.0), reads=[rfb], writes=[rfb])
        S.add("act", lambda: nc.scalar.activation(out=fb[:], in_=fb[:], func=AF.Ln, bias=1.0, scale=1.0), reads=[rfb], writes=[rfb])
        S.add("dve", lambda: nc.vector.tensor_scalar(out=fb[:], in0=fb[:], scalar1=-1.0, scalar2=None, op0=ALU.mult), reads=[rfb], writes=[rfb])
        S.add("dve", lambda: nc.vector.memset(rm[:], 1.0), writes=[rrm])
        S.add("dve", lambda: nc.vector.memset(rm[:, 0:T:128], 0.0), reads=[rrm], writes=[rrm])
        S.add("dve", lambda: nc.vector.tensor_tensor_scan(out=bb[:], data0=rm[:], data1=fb[:], initial=0.0, op0=ALU.mult, op1=ALU.add), reads=[rrm, rfb], writes=[rbb])
        S.add("dve", lambda: nc.vector.tensor_scalar(out=ib[:], in0=ib[:], scalar1=sm[0:4, go:go + 1], scalar2=None, op0=ALU.add), reads=[rib, rs_], writes=[rib])
        S.add("dve", lambda: nc.vector.tensor_tensor(out=ib[:], in0=ib[:], in1=bb[:], op=ALU.subtract), reads=[rib, rbb], writes=[rib])
        for n in range(16):
            S.add("pe", lambda n=n: nc.tensor.matmul(ps0[:, n * 4:(n + 1) * 4], lhsT=ib[0:4, n * 128:(n + 1) * 128], rhs=id4[0:4, 0:4], start=True, stop=True),
                  reads=[rib, rid], writes=[rps0])
        S.add("act", lambda: nc.scalar.copy(out=c1[:], in_=ps0[:, 0:64]), reads=[rps0], writes=[rc1])
        S.add("sp", lambda: nc.sync.dma_start(out=C1COL, in_=c1[:]), reads=[rc1], dma=True)
        S.add("sp", lambda: nc.sync.dma_start(out=BROW, in_=bb[:]), reads=[rbb], dma=True)
        kb.end()


def cdml_phase(kb, PROJ, VT, Y, BROW, C1COL, d):
    nc, S = kb.nc, kb.S
    T = SEQ
    sm = kb.smalls
    def smc(name, j):
        o = SM[name][0] + j
        return sm[:, o:o + 1]
    with ExitStack() as es:
        kb.begin()
        rs_ = kb.r_smalls
        ps = [es.enter_context(nc.psum_tensor(f"pm{i}", [128, 512], F32)) for i in range(7)]
        rps = [Res() for _ in range(7)]
        pT = es.enter_context(nc.psum_tensor("pT", [128, 1024], BF16)); rpT = Res()
        cdc = es.enter_context(nc.sbuf_tensor("cdc", [128, CDC_N], F32)); rcdc = Res()
        S.add("sp", lambda: nc.sync.dma_start(out=cdc[:], in_=d["cdc"]), writes=[rcdc], dma=True)
        identb = es.enter_context(nc.sbuf_tensor("identb", [128, 128], BF16)); rident = Res()
        S.add("dve", lambda: nc.vector.tensor_copy(out=identb[:], in_=cdc[:, C_IDENT:C_IDENT + 128]), reads=[rcdc], writes=[rident])
        brow = es.enter_context(nc.sbuf_tensor("brow", [4, T], F32)); rbrow = Res()
        S.add("sp", lambda: nc.sync.dma_start(out=brow[:], in_=BROW), writes=[rbrow], dma=True)
        c1c = es.enter_context(nc.sbuf_tensor("c1c", [128, 64], F32)); rc1c = Res()
        S.add("sp", lambda: nc.sync.dma_start(out=c1c[:], in_=C1COL), writes=[rc1c], dma=True)
        xps = Slots(nc, es, "xp", [128, T + 4], F32, 2)
        for i in range(2):
            S.add("dve", lambda i=i: nc.vector.memset(xps.t[i][:, 0:4], 0.0), writes=[xps.r[i]])
        xcs = Slots(nc, es, "xc", [128, T], F32, 2)
        qkb = [es.enter_context(nc.sbuf_tensor(f"qkb{i}", [128, T], BF16)) for i in range(4)]; rqkb = [Res() for _ in range(4)]
        qs = [es.enter_context(nc.sbuf_tensor(f"qs{i}", [128, T], BF16)) for i in range(2)]; rqs = [Res() for _ in range(2)]
        osg = [es.enter_context(nc.sbuf_tensor(f"osg{i}", [128, T], F32)) for i in range(2)]; rosg = [Res() for _ in range(2)]
        vx = es.enter_context(nc.sbuf_tensor("vx", [128, 16, 384], BF16)); rvx = Res()
        S.add("dve", lambda: nc.vector.memset(vx[:, :, 256:384], 1.0), writes=[rvx])
        eb = es.enter_context(nc.sbuf_tensor("eb", [128, T], F32))
        bm = es.enter_context(nc.sbuf_tensor("bm", [128, T], F32))
        st = es.enter_context(nc.sbuf_tensor("st", [128, T], BF16))
        wk = es.enter_context(nc.sbuf_tensor("wk", [128, 16, 256], BF16))
        hc = [es.enter_context(nc.sbuf_tensor(f"hc{i}", [128, T], F32)) for i in range(2)]
        wg = es.enter_context(nc.sbuf_tensor("wg", [128, 16], F32)); rwg = Res()
        ebl = es.enter_context(nc.sbuf_tensor("ebl", [128, 16], F32)); rebl = Res()
        cst = [es.enter_context(nc.sbuf_tensor(f"cst{i}", [128, 384], F32)) for i in range(2)]
        cbf = es.enter_context(nc.sbuf_tensor("cbf", [128, 30, 384], BF16))
        t1s = Slots(nc, es, "t1", [128, 128], F32, 4)
        yos = Slots(nc, es, "yo", [128, T], BF16, 2)
        reb = [Res() for _ in range(4)]
        rbm = [Res() for _ in range(4)]
        rst = [Res() for _ in range(4)]
        rwk = [Res() for _ in range(16)]
        rhc = [[Res() for _ in range(16)] for _ in range(2)]
        rcst = [Res(), Res()]
        rcbf = [[Res(), Res()] for _ in range(15)]
        for hh in range(4):
            for w_ in range(2):
                for dkc in range(2):
                    cc = w_ * 8 + hh * 2 + dkc
                    xp, rxp = xps.next(); xc, rxc = xcs.next()
                    S.add("sp", lambda: nc.sync.dma_start(out=xp[:, 4:4 + T], in_=PROJ[cc * 128:(cc + 1) * 128, :]), reads=[rxp], writes=[rxp], dma=True)
                    S.add("dve", lambda: nc.vector.tensor_scalar(out=xc[:], in0=xp[:, 4:4 + T], scalar1=smc("cw2", cc * 4 + 3), scalar2=smc("cb2", cc),
                                                                 op0=ALU.mult, op1=ALU.add), reads=[rxp, rs_], writes=[rxc])
                    for j in range(3):
                        S.add("dve", lambda: nc.vector.scalar_tensor_tensor(out=xc[:], in0=xp[:, 1 + j:1 + j + T], scalar=smc("cw2", cc * 4 + j), in1=xc[:],
                                                                            op0=ALU.mult, op1=ALU.add), reads=[rxp, rs_, rxc], writes=[rxc])
                    qi = w_ * 2 + dkc
                    S.add("act", lambda: nc.scalar.activation(out=qkb[qi][:], in_=xc[:], func=AF.Silu), reads=[rxc], writes=[rqkb[qi]])
            qb, kbf = qkb[0:2], qkb[2:4]
            rqb, rkb = rqkb[0:2], rqkb[2:4]
            for dvc in range(2):
                S.add("sp", lambda: nc.sync.dma_start(out=osg[dvc][:], in_=PROJ[2048 + hh * 256 + dvc * 128:2048 + hh * 256 + (dvc + 1) * 128, :]), writes=[rosg[dvc]], dma=True)
                S.add("act", lambda: nc.scalar.activation(out=osg[dvc][:], in_=osg[dvc][:], func=AF.Sigmoid), reads=[rosg[dvc]], writes=[rosg[dvc]])
            S.add("sp", lambda: nc.sync.dma_start(out=vx[:, :, 0:256], in_=VT.rearrange("(n i) f -> i n f", i=128)[:, :, hh * 256:(hh + 1) * 256]), reads=[rvx], writes=[rvx], dma=True)
            for tt in range(4):
                sl = slice(tt * 512, (tt + 1) * 512)
                S.add("pe", lambda: nc.tensor.matmul(ps[tt][:], lhsT=cdc[0:4, C_SEL + hh * 128:C_SEL + (hh + 1) * 128], rhs=brow[0:4, sl], start=True, stop=True),
                      reads=[rcdc, rbrow], writes=[rps[tt]])
                S.add("act", lambda: nc.scalar.activation(out=eb[:, sl], in_=ps[tt][:], func=AF.Exp), reads=[rps[tt]], writes=[reb[tt]])
                S.add("dve", lambda: nc.vector.tensor_tensor(out=bm[:, sl], in0=ps[tt][:], in1=cdc[:, C_M4T:C_M4T + 512], op=ALU.add), reads=[rps[tt], rcdc], writes=[rbm[tt]])
            S.add("dve", lambda: nc.vector.tensor_copy(out=ebl[:], in_=eb[:, 127:T:128]), reads=reb, writes=[rebl])
            S.add("dve", lambda: nc.vector.tensor_tensor(out=wg[:], in0=c1c[:, hh:64:4], in1=bm[:, 127:T:128], op=ALU.add), reads=[rc1c] + rbm, writes=[rwg])
            S.add("act", lambda: nc.scalar.activation(out=wg[:], in_=wg[:], func=AF.Exp), reads=[rwg], writes=[rwg])
            for dkc in range(2):
                S.add("dve", lambda: nc.vector.tensor_tensor(out=qs[dkc][:], in0=qb[dkc][:], in1=eb[:], op=ALU.mult), reads=[rqb[dkc]] + reb, writes=[rqs[dkc]])
            for g in range(4):
                for i in range(4):
                    n = g * 4 + i
                    sl = slice(n * 128, (n + 1) * 128)
                    S.add("act", lambda: nc.scalar.activation(out=bm[:, sl], in_=bm[:, sl], func=AF.Exp, bias=c1c[:, n * 4 + hh:n * 4 + hh + 1], scale=1.0),
                          reads=[rbm[g], rc1c], writes=[rbm[g]])
                    for dkc in range(2):
                        S.add("pe", lambda: nc.tensor.matmul(ps[g][:, i * 128:(i + 1) * 128], lhsT=kbf[dkc][:, sl], rhs=qb[dkc][:, sl], start=(dkc == 0), stop=(dkc == 1)),
                              reads=[rkb[dkc], rqb[dkc]], writes=[rps[g]])
                gs = slice(g * 512, (g + 1) * 512)
                S.add("dve", lambda: nc.vector.scalar_tensor_tensor(out=st[:, gs], in0=ps[g][:], scalar=0.0625, in1=bm[:, gs], op0=ALU.mult, op1=ALU.mult),
                      reads=[rps[g], rbm[g]], writes=[rst[g]])
            for g in range(4):
                for i in range(4):
                    n = g * 4 + i
                    for dkc in range(2):
                        S.add("pe", lambda: nc.tensor.transpose(out=pT[:, i * 256 + dkc * 128:i * 256 + (dkc + 1) * 128], in_=kbf[dkc][:, n * 128:(n + 1) * 128], identity=identb[:]),
                              reads=[rkb[dkc], rident], writes=[rpT])
                for i in range(4):
                    n = g * 4 + i
                    S.add("dve", lambda: nc.vector.tensor_scalar(out=wk[:, n, :], in0=pT[:, i * 256:(i + 1) * 256], scalar1=wg[:, n:n + 1], scalar2=0.0625,
                                                                 op0=ALU.mult, op1=ALU.mult), reads=[rpT, rwg], writes=[rwk[n]])
            for dkc in range(2):
                S.add("dve", lambda: nc.vector.memset(cst[dkc][:], 0.0), writes=[rcst[dkc]])
            cl = 0
            for n in range(15):
                for dkc in range(2):
                    b = 4 + (cl % 3)
                    cl += 1
                    S.add("pe", lambda: nc.tensor.matmul(ps[b][:, 0:384], lhsT=wk[:, n, dkc * 128:(dkc + 1) * 128], rhs=vx[:, n, :], start=True, stop=True),
                          reads=[rwk[n], rvx], writes=[rps[b]])
                    S.add("dve", lambda: nc.vector.scalar_tensor_tensor(out=cst[dkc][:], in0=cst[dkc][:], scalar=ebl[:, n:n + 1], in1=ps[b][:, 0:384], op0=ALU.mult, op1=ALU.add),
                          reads=[rcst[dkc], rebl, rps[b]], writes=[rcst[dkc]])
                    S.add("act", lambda: nc.scalar.copy(out=cbf[:, n * 2 + dkc, :], in_=cst[dkc][:]), reads=[rcst[dkc]], writes=[rcbf[n][dkc]])
            for n in range(16):
                sl = slice(n * 128, (n + 1) * 128)
                g = n // 4
                bnd = 4 + (n % 3)
                mm = [(slice(0, 128), vx[:, n, 0:128], rvx, st[:, sl], rst[g]),
                      (slice(128, 256), vx[:, n, 128:256], rvx, st[:, sl], rst[g]),
                      (slice(256, 384), kb.ones_bf[:], kb.r_ones, st[:, sl], rst[g])]
                if n > 0:
                    for dkc in range(2):
                        for cs in range(3):
                            mm.append((slice(cs * 128, (cs + 1) * 128), cbf[:, (n - 1) * 2 + dkc, cs * 128:(cs + 1) * 128], rcbf[n - 1][dkc], qs[dkc][:, sl], rqs[dkc]))
                for idx, (cols_, lv, rlv, rhs, rrhs) in enumerate(mm):
                    S.add("pe", lambda: nc.tensor.matmul(ps[bnd][:, cols_], lhsT=lv, rhs=rhs, start=(idx == 0), stop=(idx == len(mm) - 1), skip_group_check=True),
                          reads=[rlv, rrhs], writes=[rps[bnd]])
                t1, rt1 = t1s.next()
                S.add("act", lambda: nc.scalar.activation(out=t1[:], in_=ps[bnd][:, 256:384], func=AF.Abs), reads=[rps[bnd]], writes=[rt1])
                S.add("dve", lambda: nc.vector.tensor_scalar(out=t1[:], in0=t1[:], scalar1=1.0, scalar2=None, op0=ALU.max), reads=[rt1], writes=[rt1])
                S.add("dve", lambda: nc.vector.reciprocal(out=t1[:], in_=t1[:]), reads=[rt1], writes=[rt1])
                for dvc in range(2):
                    S.add("dve", lambda: nc.vector.tensor_tensor(out=hc[dvc][:, sl], in0=ps[bnd][:, dvc * 128:(dvc + 1) * 128], in1=t1[:], op=ALU.mult),
                          reads=[rps[bnd], rt1], writes=[rhc[dvc][n]])
            for dvc in range(2):
                S.add("act", lambda: nc.scalar.activation(out=st[:], in_=hc[dvc][:], func=AF.Square), reads=rhc[dvc], writes=rst)
                for tt in range(4):
                    S.add("pe", lambda: nc.tensor.matmul(ps[tt][:], lhsT=kb.ones_bf[:], rhs=st[:, tt * 512:(tt + 1) * 512], start=(dvc == 0), stop=(dvc == 1)),
                          reads=rst + [kb.r_ones], writes=[rps[tt]])
            for tt in range(4):
                S.add("act", lambda: nc.scalar.activation(out=bm[:, tt * 512:(tt + 1) * 512], in_=ps[tt][:], func=AF.Ln, scale=1.0 / 256, bias=EPS), reads=[rps[tt]], writes=[rbm[tt]])
            S.add("act", lambda: nc.scalar.activation(out=bm[:], in_=bm[:], func=AF.Exp, scale=-0.5), reads=rbm, writes=rbm)
            for dvc in range(2):
                S.add("dve", lambda: nc.vector.scalar_tensor_tensor(out=eb[:], in0=hc[dvc][:], scalar=smc("hg", hh * 2 + dvc), in1=bm[:], op0=ALU.mult, op1=ALU.mult),
                      reads=rhc[dvc] + rbm + [rs_], writes=reb)
                yo, ryo = yos.next()
                S.add("dve", lambda: nc.vector.tensor_tensor(out=yo[:], in0=eb[:], in1=osg[dvc][:], op=ALU.mult), reads=reb + [rosg[dvc]], writes=[ryo])
                S.add("sp", lambda: nc.sync.dma_start(out=Y[hh * 256 + dvc * 128:hh * 256 + (dvc + 1) * 128, :], in_=yo[:]), reads=[ryo], dma=True)
        kb.end()


def skewed(units, stages):
    n, k = len(units), len(stages)
    for t in range(n + k - 1):
        for s_ in reversed(range(k)):
            u = t - s_
            if 0 <= u < n:
                stages[s_](units[u])


def cdsb_phase(kb, QKD, VT, Y, d):
    nc, S = kb.nc, kb.S
    T = SEQ
    with ExitStack() as es:
        kb.begin()
        ps = [es.enter_context(nc.psum_tensor(f"pz{i}", [128, 512], F32)) for i in range(6)]
        rps = [Res() for _ in range(6)]
        pTs = [es.enter_context(nc.psum_tensor(f"pTs{i}", [128, 1024], BF16)) for i in range(2)]
        rpT = [Res(), Res()]
        cdc = es.enter_context(nc.sbuf_tensor("cdc", [128, CDC_N], F32)); rcdc = Res()
        S.add("sp", lambda: nc.sync.dma_start(out=cdc[:], in_=d["cdc"]), writes=[rcdc], dma=True)
        identb = es.enter_context(nc.sbuf_tensor("identb", [128, 128], BF16)); rident = Res()
        S.add("dve", lambda: nc.vector.tensor_copy(out=identb[:], in_=cdc[:, C_IDENT:C_IDENT + 128]), reads=[rcdc], writes=[rident])
        onesf = es.enter_context(nc.sbuf_tensor("onesf", [128, T], F32)); rof = Res()
        S.add("dve", lambda: nc.vector.memset(onesf[:], 1.0), writes=[rof])
        qks = Slots(nc, es, "qk", [128, T], BF16, 4)
        vts = Slots(nc, es, "vt", [128, 16, 128], BF16, 2)
        sps = Slots(nc, es, "sp", [128, T + 1], F32, 4)
        fs = Slots(nc, es, "F", [128, T], F32, 4)
        ats = Slots(nc, es, "at", [128, T], BF16, 3)
        aTs = Slots(nc, es, "aT", [128, T], BF16, 3)
        nts = Slots(nc, es, "nt", [128, 1], F32, 6)
        yds = Slots(nc, es, "yd", [128, T], BF16, 2)
        for i in range(4):
            S.add("dve", lambda i=i: nc.vector.memset(sps.t[i][:, 0:1], 0.0), writes=[sps.r[i]])
        zcnt = [0]
        z2cnt = [0]
        tcnt = [0]
        heads = {}

        def s0(u):
            h, n = u["h"], u["n"]
            if n == 0:
                q, rq = qks.next(); k, rk = qks.next()
                S.add("sp", lambda: nc.sync.dma_start(out=q[:], in_=QKD[h * 128:(h + 1) * 128, :]), writes=[rq], dma=True)
                S.add("sp", lambda: nc.sync.dma_start(out=k[:], in_=QKD[1024 + h * 128:1024 + (h + 1) * 128, :]), writes=[rk], dma=True)
                vt, rvt = vts.next()
                S.add("sp", lambda: nc.sync.dma_start(out=vt[:], in_=VT.rearrange("(b i) f -> i b f", i=128)[:, :, 1024 + h * 128:1024 + (h + 1) * 128]), writes=[rvt], dma=True)
                yd, ryd = yds.next()
                heads[h] = (q, rq, k, rk, vt, rvt, yd, ryd)
            q, rq, k, rk, vt, rvt, yd, ryd = heads[h]
            Kn = (n + 1) * 128
            nb = (Kn + 511) // 512
            sp_, rsp = sps.next()
            u.update(Kn=Kn, nb=nb, sp=sp_, rsp=rsp)
            for bi in range(nb):
                w_ = min(512, Kn - bi * 512)
                ksl = slice(bi * 512, bi * 512 + w_)
                b = zcnt[0] % 2
                zcnt[0] += 1
                S.add("pe", lambda: nc.tensor.matmul(ps[b][:, 0:w_], lhsT=q[:, n * 128:(n + 1) * 128], rhs=k[:, ksl], start=True, stop=True),
                      reads=[rq, rk], writes=[rps[b]])
                S.add("act", lambda: nc.scalar.activation(out=sp_[:, 1 + bi * 512:1 + bi * 512 + w_], in_=ps[b][:, 0:w_], func=AF.Exp), reads=[rps[b]], writes=[rsp])

        def s0b(u):
            n, Kn, sp_, rsp = u["n"], u["Kn"], u["sp"], u["rsp"]
            S.add("act", lambda: nc.scalar.activation(out=sp_[:, 1:1 + Kn], in_=sp_[:, 1:1 + Kn], func=AF.Ln, bias=1.0, scale=1.0), reads=[rsp], writes=[rsp])
            S.add("dve", lambda: nc.vector.tensor_tensor(out=sp_[:, 1 + n * 128:1 + Kn], in0=sp_[:, 1 + n * 128:1 + Kn], in1=cdc[:, C_TRIL:C_TRIL + 128], op=ALU.mult),
                  reads=[rsp, rcdc], writes=[rsp])

        def s1a(u):
            n, Kn, sp_, rsp = u["n"], u["Kn"], u["sp"], u["rsp"]
            F, rF = fs.next()
            u.update(F=F, rF=rF)
            S.add("dve", lambda: nc.vector.tensor_tensor_scan(out=F[:, 0:Kn], data0=onesf[:, 0:Kn], data1=sp_[:, 0:Kn], initial=0.0, op0=ALU.mult, op1=ALU.add),
                  reads=[rof, rsp], writes=[rF])

        def s1(u):
            h, n, Kn, nb, F, rF = u["h"], u["n"], u["Kn"], u["nb"], u["F"], u["rF"]
            q, rq, k, rk, vt, rvt, yd, ryd = heads[h]
            nt, rnt = nts.next()
            u.update(nt=nt, rnt=rnt)
            S.add("dve", lambda: nc.vector.tensor_scalar(out=nt[:], in0=F[:, Kn - 1:Kn], scalar1=-1.0, scalar2=None, op0=ALU.mult), reads=[rF], writes=[rnt])
            for bi in range(nb):
                w_ = min(512, Kn - bi * 512)
                ksl = slice(bi * 512, bi * 512 + w_)
                b = 2 + (z2cnt[0] % 2)
                z2cnt[0] += 1
                S.add("pe", lambda: nc.tensor.matmul(ps[b][:, 0:w_], lhsT=q[:, n * 128:(n + 1) * 128], rhs=k[:, ksl], start=True, stop=True),
                      reads=[rq, rk], writes=[rps[b]])
                S.add("dve", lambda: nc.vector.tensor_tensor(out=F[:, ksl], in0=ps[b][:, 0:w_], in1=F[:, ksl], op=ALU.add), reads=[rps[b], rF, rnt], writes=[rF])
            S.add("dve", lambda: nc.vector.tensor_tensor(out=F[:, n * 128:Kn], in0=F[:, n * 128:Kn], in1=cdc[:, C_NEGSB:C_NEGSB + 128], op=ALU.add),
                  reads=[rF, rcdc], writes=[rF])

        def s2(u):
            Kn, F, rF, nt, rnt = u["Kn"], u["F"], u["rF"], u["nt"], u["rnt"]
            at, rat = ats.next()
            u.update(at=at, rat=rat)
            S.add("act", lambda: nc.scalar.activation(out=at[:, 0:Kn], in_=F[:, 0:Kn], func=AF.Exp, bias=nt[:, 0:1], scale=1.0), reads=[rF, rnt], writes=[rat])

        def s3(u):
            n, at, rat = u["n"], u["at"], u["rat"]
            aT, raT = aTs.next()
            u.update(aT=aT, raT=raT)
            for g in range((n + 8) // 8):
                tb = tcnt[0] % 2
                tcnt[0] += 1
                k0, k1 = g * 8, min(n + 1, g * 8 + 8)
                for kk in range(k0, k1):
                    S.add("pe", lambda: nc.tensor.transpose(out=pTs[tb][:, (kk - k0) * 128:(kk - k0 + 1) * 128], in_=at[:, kk * 128:(kk + 1) * 128],
                                                            identity=identb[:]), reads=[rat, rident], writes=[rpT[tb]])
                wdt = (k1 - k0) * 128
                if tb == 0:
                    S.add("act", lambda: nc.scalar.copy(out=aT[:, k0 * 128:k0 * 128 + wdt], in_=pTs[tb][:, 0:wdt]), reads=[rpT[tb]], writes=[raT])
                else:
                    S.add("dve", lambda: nc.vector.tensor_copy(out=aT[:, k0 * 128:k0 * 128 + wdt], in_=pTs[tb][:, 0:wdt]), reads=[rpT[tb]], writes=[raT])

        def s4(u):
            h, n, aT, raT = u["h"], u["n"], u["aT"], u["raT"]
            q, rq, k, rk, vt, rvt, yd, ryd = heads[h]
            ob = 4 + ((n // 4) % 2)
            oc = slice((n % 4) * 128, (n % 4 + 1) * 128)
            for kk in range(n + 1):
                S.add("pe", lambda: nc.tensor.matmul(ps[ob][:, oc], lhsT=vt[:, kk, :], rhs=aT[:, kk * 128:(kk + 1) * 128], start=(kk == 0), stop=(kk == n)),
                      reads=[rvt, raT], writes=[rps[ob]])
            if n % 4 == 3:
                g4 = n // 4
                S.add("dve", lambda: nc.vector.tensor_copy(out=yd[:, g4 * 512:(g4 + 1) * 512], in_=ps[ob][:]), reads=[rps[ob]], writes=[ryd])
            if n == 15:
                S.add("sp", lambda: nc.sync.dma_start(out=Y[1024 + h * 128:1024 + (h + 1) * 128, :], in_=yd[:]), reads=[ryd], dma=True)

        units = [dict(h=h, n=n) for h in range(8) for n in range(16)]
        skewed(units, [s0, s0b, s1a, s1, s2, s3, s4])
        kb.end()


def cd_phases(kb, cur, curkey, dst, dkey, gcol, d, PROJ, VT, Y, QKD, GIF, BROW, C1COL, RSTD, gn):
    w = d["cd_w_in"][0]
    fm = [(c * 128, 128, PROJ[c * 128:(c + 1) * 128, :], F32, 1.0) for c in range(16)]
    fm += [(3072 + c * 128, 128, PROJ[2048 + c * 128:2048 + (c + 1) * 128, :], F32, 1.0) for c in range(8)]
    fm += [(4096, 4, GIF[0:4, :], F32, 1.0), (4100, 4, GIF[4:8, :], F32, 1.0)]
    fm += [(4104 + c * 128, 128, QKD[c * 128:(c + 1) * 128, :], BF16, 128.0 ** -0.5) for c in range(8)]
    fm += [(5128 + c * 128, 128, QKD[1024 + c * 128:1024 + (c + 1) * 128, :], BF16, 1.0) for c in range(8)]
    tm = [(2048 + g * 512, VT[:, g * 512:(g + 1) * 512]) for g in range(2)]
    tm += [(6152 + g * 512, VT[:, 1024 + g * 512:1024 + (g + 1) * 512]) for g in range(2)]
    proj_phase(kb, cur, curkey, gcol, w, fm, tm, RSTD)
    if "gate" in CD_SUB:
        cdgate_phase(kb, GIF, BROW, C1COL, d)
    if "ml" in CD_SUB:
        cdml_phase(kb, PROJ, VT, Y, BROW, C1COL, d)
    if "sb" in CD_SUB:
        cdsb_phase(kb, QKD, VT, Y, d)
    outproj_phase(kb, cur, curkey, dst, dkey, Y, d["cd_w_out"][0], RSTD, g_next=gn)


def kernel(**inp):
    phases = tuple(ALL_PHASES)
    x = np.asarray(inp["x"], np.float32)
    sm = build_smalls(inp)
    nc = build_program(sm.shape[1], phases)
    shared = {"smalls": sm}
    f32c = lambda k: np.ascontiguousarray(np.asarray(inp[k], np.float32))
    if any(p.startswith("ffn") for p in phases):
        shared["ffn_w_in"] = f32c("ffn_w_in"); shared["ffn_w_out"] = f32c("ffn_w_out")
    if "ab" in phases:
        for k in ("ab_w_in", "rg_w_r", "rg_w_i", "rel_bias", "ab_w_out"):
            shared[k] = f32c(k)
        shared["selT"] = t5_sel_table()
    if "cd" in phases:
        for k in ("cd_w_in", "cd_w_out"):
            shared[k] = f32c(k)
        shared["cdc"] = cd_consts()
    in_maps = []
    for b in range(NCORES):
        m = dict(shared)
        m["xT"] = np.ascontiguousarray(x[b].T)
        in_maps.append(m)
    res = run_bass_kernel_spmd(nc, in_maps, core_ids=list(range(NCORES)))
    kernel.last_results = res
    out = np.stack([np.ascontiguousarray(np.asarray(r["outT"]).T) for r in res.results], axis=0)
    return out.astype(np.float32)
```
